# Optimizing a Trainium2 kernel written in Bass

```python
import math, functools
import jax, jax.numpy as jnp
from jax import lax
import numpy as np

D_MODEL = 1024
BATCH = 8
SEQ = 2048
DEPTH = 2
DEC_BATCH = 128
DEC_SEQ = 4
PAST_LEN = 2048
PAGE_SIZE = 128

N_BRANCH = 4
BRANCH_W = D_MODEL // 2
POOL_WINDOWS = (2, 4, 8, 16)
POOL_GROUPS = len(POOL_WINDOWS)
POOL_GC = BRANCH_W // POOL_GROUPS
POOL_BUF = max(POOL_WINDOWS) - 1
SCONV_K = 3
CCONV_K = 31
HEAD_DK = 64
HEAD_DV = 2 * HEAD_DK
N_HEADS = BRANCH_W // HEAD_DV
ATTN_SCALE = HEAD_DK ** -0.5
ROPE_THETA = 10000.0
Q_BLOCK = 128
EPS = 1e-6
IN_SIZES = (BRANCH_W, 3 * BRANCH_W, 2 * BRANCH_W, N_HEADS * 2 * HEAD_DK, N_HEADS * 2 * HEAD_DK, N_HEADS * HEAD_DV, N_BRANCH * BRANCH_W, N_BRANCH * D_MODEL)
N_IN = sum(IN_SIZES)

kernel_name = 'hybrid_pool_conv_diffattn_step'


def _split(t, sizes):
    out, o = [], 0
    for s in sizes:
        out.append(t[..., o:o + s])
        o += s
    return out


def _rms(x, g):
    xf = x.astype(jnp.float32)
    y = xf * lax.rsqrt(jnp.mean(xf * xf, axis=-1, keepdims=True) + EPS)
    return (y * g.astype(jnp.float32)).astype(x.dtype)


def _layernorm(x, g, b):
    xf = x.astype(jnp.float32)
    mu = jnp.mean(xf, axis=-1, keepdims=True)
    xc = xf - mu
    y = xc * lax.rsqrt(jnp.mean(xc * xc, axis=-1, keepdims=True) + EPS)
    return (y * g.astype(jnp.float32) + b.astype(jnp.float32)).astype(x.dtype)


def _rope(t, pos):
    half = HEAD_DK // 2
    inv = jnp.power(ROPE_THETA, -jnp.arange(half, dtype=jnp.float32) / half)
    ang = pos.astype(jnp.float32)[:, None] * inv[None, :]
    cos = jnp.cos(ang)[None, :, None, None, :]
    sin = jnp.sin(ang)[None, :, None, None, :]
    tf = t.astype(jnp.float32)
    t1, t2 = tf[..., :half], tf[..., half:]
    return jnp.concatenate([t1 * cos - t2 * sin, t2 * cos + t1 * sin], axis=-1).astype(t.dtype)


def _causal_dwconv(u, prefix, w):
    k_w, ch = w.shape
    uu = jnp.concatenate([prefix.astype(u.dtype), u], axis=1)
    out = lax.conv_general_dilated(uu, w[:, None, :].astype(u.dtype), window_strides=(1,), padding='VALID',
                                   dimension_numbers=('NWC', 'WIO', 'NWC'), feature_group_count=ch)
    return out, uu[:, uu.shape[1] - (k_w - 1):]


def _pool_mixer(u, prefix, pos, w_pool, pool_scale):
    b, s, w_ch = u.shape
    uu = jnp.concatenate([prefix.astype(u.dtype), u], axis=1)
    cs = jnp.cumsum(uu.astype(jnp.float32), axis=1)
    cs = jnp.concatenate([jnp.zeros((b, 1, w_ch), jnp.float32), cs], axis=1)
    end = cs[:, POOL_BUF + 1:]
    means = []
    for g, win in enumerate(POOL_WINDOWS):
        sl = slice(g * POOL_GC, (g + 1) * POOL_GC)
        start = cs[:, POOL_BUF + 1 - win:POOL_BUF + 1 - win + s, sl]
        cnt = jnp.minimum(win, pos + 1).astype(jnp.float32)[None, :, None]
        means.append((end[..., sl] - start) / cnt)
    mean = jnp.stack(means, axis=2)
    d = (mean - u.reshape(b, s, POOL_GROUPS, POOL_GC).astype(jnp.float32)).astype(u.dtype)
    y = jnp.einsum('bsgc,gce->bsge', d, w_pool).reshape(b, s, w_ch) * pool_scale
    return y, uu[:, uu.shape[1] - POOL_BUF:]


def _diff_weights(s, lam):
    p = jax.nn.softmax(s, axis=-1)
    return p[:, :, 0] - lam * p[:, :, 1]


def _attend_prompt(q, k, v, lam):
    b, s = q.shape[:2]
    nb = s // Q_BLOCK
    qb = jnp.moveaxis(q.reshape(b, nb, Q_BLOCK, N_HEADS, 2, HEAD_DK), 1, 0)
    kpos = jnp.arange(s)

    def block(args):
        qi, bi = args
        sc = jnp.einsum('bqhjd,bkhjd->bhjqk', qi, k).astype(jnp.float32) * ATTN_SCALE
        qpos = bi * Q_BLOCK + jnp.arange(Q_BLOCK)
        sc = jnp.where(kpos[None, :] <= qpos[:, None], sc, -jnp.inf)
        wts = _diff_weights(sc, lam).astype(v.dtype)
        return jnp.einsum('bhqk,bkhe->bqhe', wts, v)

    o = lax.map(block, (qb, jnp.arange(nb)))
    return jnp.moveaxis(o, 0, 1).reshape(b, s, N_HEADS, HEAD_DV)


def _attend_sample(q, k, v, lam, k_past, v_past):
    t = q.shape[1]
    p_len = k_past.shape[1]
    s_past = jnp.einsum('bqhjd,bkhjd->bhjqk', q, k_past).astype(jnp.float32) * ATTN_SCALE
    s_new = jnp.einsum('bqhjd,bkhjd->bhjqk', q, k).astype(jnp.float32) * ATTN_SCALE
    causal = jnp.tril(jnp.ones((t, t), dtype=bool))
    sc = jnp.concatenate([s_past, jnp.where(causal, s_new, -jnp.inf)], axis=-1)
    wts = _diff_weights(sc, lam).astype(v.dtype)
    return (jnp.einsum('bhqk,bkhe->bqhe', wts[..., :p_len], v_past)
            + jnp.einsum('bhqk,bkhe->bqhe', wts[..., p_len:], v))


def _layer(x, c, pos, pool_pre, sconv_pre, cconv_pre, attend, lam_init,
           w_ada, b_ada, g_pre, g_post, w_in, w_pool, pool_scale, w_sconv, w_cconv,
           b_cconv, g_cnorm, b_cnorm, lambda_qk, g_subln, w_branch, w_o):
    b, s, _ = x.shape
    mod = jax.nn.silu(c) @ w_ada + b_ada
    shift, scale, gate = jnp.split(mod, 3, axis=-1)
    h = _rms(x, g_pre) * (1.0 + scale[:, None]) + shift[:, None]
    u_a, bch, glu, q, k, v, mgate, mrg = _split(h @ w_in, IN_SIZES)
    y_a, pool_new = _pool_mixer(u_a, pool_pre, pos, w_pool, pool_scale)
    b_g, c_g, h_b = _split(bch, (BRANCH_W, BRANCH_W, BRANCH_W))
    conv_b, sconv_new = _causal_dwconv(c_g * h_b, sconv_pre, w_sconv)
    y_b = b_g * conv_b
    val, gl = _split(glu, (BRANCH_W, BRANCH_W))
    conv_c, cconv_new = _causal_dwconv(val * jax.nn.sigmoid(gl), cconv_pre, w_cconv)
    y_c = jax.nn.silu(_layernorm(conv_c + b_cconv, g_cnorm, b_cnorm))
    q = _rope(q.reshape(b, s, N_HEADS, 2, HEAD_DK), pos)
    k = _rope(k.reshape(b, s, N_HEADS, 2, HEAD_DK), pos)
    v = v.reshape(b, s, N_HEADS, HEAD_DV)
    lq = lambda_qk.astype(jnp.float32)
    lam = jnp.exp(jnp.sum(lq[0] * lq[1])) - jnp.exp(jnp.sum(lq[2] * lq[3])) + lam_init
    o = _rms(attend(q, k, v, lam), g_subln) * (1.0 - lam_init)
    y_d = o.reshape(b, s, BRANCH_W)
    ys = jnp.stack([y_a, y_b, y_c, y_d], axis=2) * jax.nn.silu(mgate.reshape(b, s, N_BRANCH, BRANCH_W))
    br = jnp.einsum('bsnw,nwd->bsnd', ys, w_branch)
    merged = jnp.sum(jax.nn.sigmoid(mrg.reshape(b, s, N_BRANCH, D_MODEL)) * br, axis=2)
    out = _rms(merged @ w_o, g_post)
    return x + gate[:, None] * out, k, v, pool_new, sconv_new, cconv_new


def setup_inputs(seed: int = 0) -> dict:
    key = jax.random.key(seed)
    ks = iter(jax.random.split(key, 32))

    def nrm(shape, sc):
        return jax.random.normal(next(ks), shape, jnp.float32) * sc

    n_pages = PAST_LEN // PAGE_SIZE
    n_used = DEC_BATCH * n_pages
    n_pool = n_used + n_used // 4
    page_table = jax.random.permutation(next(ks), n_pool)[:n_used].reshape(DEC_BATCH, n_pages).astype(jnp.int32)
    return {
        'x_prompt': nrm((BATCH, SEQ, D_MODEL), 1.0),
        'x_sample': nrm((DEC_BATCH, DEC_SEQ, D_MODEL), 1.0),
        'cache_k': nrm((DEPTH, n_pool, PAGE_SIZE, N_HEADS, 2, HEAD_DK), 1.0),
        'cache_v': nrm((DEPTH, n_pool, PAGE_SIZE, N_HEADS, HEAD_DV), 1.0),
        'page_table': page_table,
        'state_pool': nrm((DEPTH, DEC_BATCH, POOL_BUF, BRANCH_W), 1.0),
        'state_sconv': nrm((DEPTH, DEC_BATCH, SCONV_K - 1, BRANCH_W), 1.0),
        'state_cconv': nrm((DEPTH, DEC_BATCH, CCONV_K - 1, BRANCH_W), 0.5),
        'c_prompt': nrm((BATCH, D_MODEL), 1.0),
        'c_sample': nrm((DEC_BATCH, D_MODEL), 1.0),
        'w_ada': nrm((DEPTH, D_MODEL, 3 * D_MODEL), 0.5 * D_MODEL ** -0.5),
        'b_ada': nrm((DEPTH, 3 * D_MODEL), 0.01),
        'g_pre': 1.0 + nrm((DEPTH, D_MODEL), 0.05),
        'g_post': 1.0 + nrm((DEPTH, D_MODEL), 0.05),
        'w_in': nrm((DEPTH, D_MODEL, N_IN), D_MODEL ** -0.5),
        'w_pool': nrm((DEPTH, POOL_GROUPS, POOL_GC, POOL_GC), POOL_GC ** -0.5),
        'pool_scale': 1.0 + nrm((DEPTH, BRANCH_W), 0.1),
        'w_sconv': nrm((DEPTH, SCONV_K, BRANCH_W), SCONV_K ** -0.5),
        'w_cconv': nrm((DEPTH, CCONV_K, BRANCH_W), CCONV_K ** -0.5),
        'b_cconv': nrm((DEPTH, BRANCH_W), 0.01),
        'g_cnorm': 1.0 + nrm((DEPTH, BRANCH_W), 0.05),
        'b_cnorm': nrm((DEPTH, BRANCH_W), 0.01),
        'lambda_qk': nrm((DEPTH, 4, HEAD_DK), 0.1),
        'g_subln': 1.0 + nrm((DEPTH, HEAD_DV), 0.05),
        'w_branch': nrm((DEPTH, N_BRANCH, BRANCH_W, D_MODEL), BRANCH_W ** -0.5),
        'w_o': nrm((DEPTH, D_MODEL, D_MODEL), D_MODEL ** -0.5),
    }


def reference(x_prompt, x_sample, cache_k, cache_v, page_table, state_pool, state_sconv, state_cconv,
              c_prompt, c_sample, w_ada, b_ada, g_pre, g_post, w_in, w_pool, pool_scale, w_sconv,
              w_cconv, b_cconv, g_cnorm, b_cnorm, lambda_qk, g_subln, w_branch, w_o):
    bp, sp = x_prompt.shape[:2]
    db, ds = x_sample.shape[:2]
    past = page_table.shape[1] * cache_k.shape[2]
    pos_p = jnp.arange(sp)
    pos_s = past + jnp.arange(ds)
    z_pool = jnp.zeros((bp, POOL_BUF, BRANCH_W), x_prompt.dtype)
    z_sconv = jnp.zeros((bp, SCONV_K - 1, BRANCH_W), x_prompt.dtype)
    z_cconv = jnp.zeros((bp, CCONV_K - 1, BRANCH_W), x_prompt.dtype)
    xp, xs = x_prompt, x_sample
    new_p = ([], [], [], [], [])
    new_s = ([], [], [], [], [])
    for l in range(DEPTH):
        lam_init = 0.8 - 0.6 * math.exp(-0.3 * l)
        wl = (w_ada[l], b_ada[l], g_pre[l], g_post[l], w_in[l], w_pool[l], pool_scale[l], w_sconv[l],
              w_cconv[l], b_cconv[l], g_cnorm[l], b_cnorm[l], lambda_qk[l], g_subln[l], w_branch[l], w_o[l])
        xp, kp, vp, pp, scp, ccp = _layer(xp, c_prompt, pos_p, z_pool, z_sconv, z_cconv,
                                          _attend_prompt, lam_init, *wl)
        k_past = cache_k[l, page_table].reshape(db, past, N_HEADS, 2, HEAD_DK)
        v_past = cache_v[l, page_table].reshape(db, past, N_HEADS, HEAD_DV)
        attend_s = functools.partial(_attend_sample, k_past=k_past, v_past=v_past)
        xs, ks_, vs_, ps, scs, ccs = _layer(xs, c_sample, pos_s, state_pool[l], state_sconv[l], state_cconv[l],
                                            attend_s, lam_init, *wl)
        for lst, t in zip(new_p, (kp, vp, pp, scp, ccp)):
            lst.append(t)
        for lst, t in zip(new_s, (ks_, vs_, ps, scs, ccs)):
            lst.append(t)
    k_prompt, v_prompt, pool_prompt, sconv_prompt, cconv_prompt = [jnp.stack(t) for t in new_p]
    k_sample, v_sample, pool_sample, sconv_sample, cconv_sample = [jnp.stack(t) for t in new_s]
    return (xp, xs, k_prompt, v_prompt, pool_prompt, sconv_prompt, cconv_prompt,
            k_sample, v_sample, pool_sample, sconv_sample, cconv_sample)
```

```python
import numpy as np
from contextlib import ExitStack
import concourse.bass as bass
import concourse.mybir as mybir
from concourse.bass_utils import run_bass_kernel_spmd

F32 = mybir.dt.float32
BF16 = mybir.dt.bfloat16
I32 = mybir.dt.int32
ALU = mybir.AluOpType
AF = mybir.ActivationFunctionType
AX = mybir.AxisListType

D = 1024
BW = 512
SEQ = 2048
DEPTH = 2
NSB = 16
NS = 64
N_IN = 10752
EPS = 1e-6
WINS = (2, 4, 8, 16)
LAM_INIT = [0.8 - 0.6 * float(np.exp(-0.3 * l)) for l in range(DEPTH)]


class Buf:
    def __init__(self, name, psum=False):
        self.name = name
        self.psum = psum
        self.w = None
        self.r = []
        self.dsem = None
        self.dcnt = 0


class Eng:
    def __init__(self, h, sem, is_pe=False):
        self.h = h
        self.sem = sem
        self.cnt = 0
        self.seen = {}
        self.is_pe = is_pe


class KB:
    def __init__(self, nc, es):
        self.nc = nc
        self.es = es
        self.bufs = {}
        self._keep = []
        self.nsem = 0
        self.pe = Eng(nc.tensor, self.sem("pe"), True)
        self.act = Eng(nc.scalar, self.sem("act"))
        self.dve = Eng(nc.vector, self.sem("dve"))
        self.pool = Eng(nc.gpsimd, self.sem("pool"))
        self.sp = Eng(nc.sync, self.sem("sp"))
        self.stores = []

    def sem(self, name):
        self.nsem += 1
        return self.es.enter_context(self.nc.semaphore("s%d_%s" % (self.nsem, name)))

    def sb(self, name, shape, dt):
        t = self.es.enter_context(self.nc.sbuf_tensor(name, list(shape), dt))
        self.bufs[id(t)] = Buf(name)
        self._keep.append(t)
        return t

    def ps(self, name, shape, dt):
        t = self.es.enter_context(self.nc.psum_tensor(name, list(shape), dt))
        self.bufs[id(t)] = Buf(name, psum=True)
        self._keep.append(t)
        return t

    def buf(self, t):
        if isinstance(t, Buf):
            return t
        return self.bufs[id(t)]

    def _wait(self, eng, tk):
        if tk is None:
            return
        sem, val = tk
        key = id(sem)
        if eng.seen.get(key, 0) >= val:
            return
        if eng.is_pe and sem is eng.sem:
            return
        eng.h.wait_ge(sem, val)
        eng.seen[key] = val

    def _deps(self, eng, reads, writes):
        for t in reads:
            self._wait(eng, self.buf(t).w)
        for t in writes:
            b = self.buf(t)
            self._wait(eng, b.w)
            for tk in b.r:
                self._wait(eng, tk)

    def op(self, eng, fn, reads=(), writes=(), inc=True):
        pr = [t for t in reads if self.buf(t).psum]
        if pr:
            reads = [t for t in reads if not self.buf(t).psum]
            writes = list(writes) + [t for t in pr if t not in writes]
        self._deps(eng, reads, writes)
        ins = fn(eng.h)
        if inc:
            ins.then_inc(eng.sem, 1)
            eng.cnt += 1
            tk = (eng.sem, eng.cnt)
        else:
            tk = (eng.sem, eng.cnt + 1)
        for t in reads:
            self.buf(t).r.append(tk)
        for t in writes:
            b = self.buf(t)
            b.w = tk
            b.r = []
        return tk

    def dma(self, eng, fns, reads=(), writes=(), sem_of=None, store=False):
        self._deps(eng, reads, writes)
        b = self.buf(sem_of if sem_of is not None else (writes[0] if writes else reads[0]))
        if b.dsem is None:
            b.dsem = self.sem("d_" + b.name)
        for fn in fns:
            fn(eng.h).then_inc(b.dsem, 16)
            b.dcnt += 16
        tk = (b.dsem, b.dcnt)
        for t in reads:
            self.buf(t).r.append(tk)
        for t in writes:
            bb = self.buf(t)
            bb.w = tk
            bb.r = []
        if store:
            self.stores.append(tk)
        return tk

    def view(self, name, ap):
        self.bufs[id(ap)] = Buf(name)
        self._keep.append(ap)
        return ap

    def barrier(self):
        engs = (self.pe, self.act, self.dve, self.pool, self.sp)
        dts = [(b.dsem, b.dcnt) for b in self.bufs.values() if b.dsem is not None and b.dcnt]
        dts += [(b.dsem, b.dcnt) for b in getattr(self, "xbufs", []) if b.dsem is not None and b.dcnt]
        for e in engs:
            for f in engs:
                if f is not e and f.cnt:
                    self._wait(e, (f.sem, f.cnt))
            for tk in dts:
                self._wait(e, tk)

    def finish(self):
        for tk in self.stores:
            self._wait(self.sp, tk)
        for e in (self.pe, self.act, self.dve, self.pool):
            if e.cnt:
                self._wait(self.sp, (e.sem, e.cnt))


class _Stop(Exception):
    pass


_STOP = [0]


def build(n_pool):
    st = {}
    try:
        _build_inner(n_pool, st)
    except _Stop:
        pass
    st['kb'].finish()
    return st['nc'], st['es']


def _build_inner(n_pool, st):
    nc = bass.Bass("TRN2", target_bir_lowering=False)
    es = ExitStack()
    kb = KB(nc, es)
    st.update(nc=nc, es=es, kb=kb)
    PE, ACT, DVE, POOL, SP = kb.pe, kb.act, kb.dve, kb.pool, kb.sp

    def din(name, shape, dt=F32):
        return nc.dram_tensor(name, list(shape), dt, kind="ExternalInput").ap()

    def dout(name, shape, dt=F32):
        return nc.dram_tensor(name, list(shape), dt, kind="ExternalOutput").ap()

    xp = din("xp", [SEQ, D]); xs = din("xs", [NS, D])
    ck = [din(f"ck{i}", [n_pool * 128, BW]) for i in range(DEPTH)]; cv = [din(f"cv{i}", [n_pool * 128, BW]) for i in range(DEPTH)]
    breg = es.enter_context(nc.gpsimd.register("bc"))
    nc.gpsimd.reg_mov(breg, n_pool * 128 - 1)
    ptab = din("ptab", [NSB, 16], I32)
    st_pool = din("st_pool", [DEPTH, NSB, 15, BW]); st_sconv = din("st_sconv", [DEPTH, NSB, 2, BW])
    st_cconv = din("st_cconv", [DEPTH, NSB, 30, BW])
    c_all = din("c_all", [17, D])
    w_ada = din("w_ada", [DEPTH, D, 3 * D]); b_ada = din("b_ada", [DEPTH, 3 * D])
    g_pre = din("g_pre", [DEPTH, D]); g_post = din("g_post", [DEPTH, D])
    w_in = din("w_in", [DEPTH, D, N_IN]); w_pool = din("w_pool", [DEPTH, 4, 128, 128])
    ppack = din("ppack", [DEPTH, 38, BW]); lam_qk = din("lam_qk", [DEPTH, 256])
    g_subln = din("g_subln", [DEPTH, 128])
    w_branch = din("w_branch", [DEPTH, 4, BW, D]); w_o = din("w_o", [DEPTH, D, D])
    c_ident = din("c_ident", [128, 128]); c_tri = din("c_tri", [128, 128])
    c_cs = din("c_cs", [2, 17, 128, 32]); c_pinv = din("c_pinv", [4, 16])
    c_smask = din("c_smask", [64, 128])

    y_p = dout("y_p", [SEQ, D]); y_s = dout("y_s", [NS, D])
    k_p = dout("k_p", [DEPTH, SEQ, BW]); v_p = dout("v_p", [DEPTH, SEQ, BW])
    pool_p = dout("pool_p", [DEPTH, 15, BW]); sconv_p = dout("sconv_p", [DEPTH, 2, BW])
    cconv_p = dout("cconv_p", [DEPTH, 30, BW])
    k_s = dout("k_s", [DEPTH, NS, BW]); v_s = dout("v_s", [DEPTH, NS, BW])
    pool_s = dout("pool_s", [DEPTH, NSB, 15, BW]); sconv_s = dout("sconv_s", [DEPTH, NSB, 2, BW])
    cconv_s = dout("cconv_s", [DEPTH, NSB, 30, BW])
    x1 = nc.dram_tensor("x1_scratch", [SEQ + NS, D], F32, kind="Internal").ap()
    x1buf = Buf("x1dram")

    xt = kb.sb("xt", [128, 4, D], F32)
    kT = kb.sb("kT", [128, 4, SEQ + NS], BF16)
    vtok = kb.sb("vtok", [128, 17, BW], BF16)
    NW = 2
    wb = [kb.sb(f"wb{i}", [128, 8, 512], BF16) for i in range(NW)]
    wctr = [0]
    ident = kb.sb("ident", [128, 128], F32); identb = kb.sb("identb", [128, 128], BF16)
    trib = kb.sb("trib", [128, 128], BF16); onesb = kb.sb("onesb", [128, 128], BF16)
    trif = kb.sb("trif", [128, 128], F32)
    smask = kb.sb("smask", [64, 128], BF16)
    cs = kb.sb("cs", [128, 2, 17, 32], F32)
    pinv = kb.sb("pinv", [128, 4, 16], F32)
    epsb = kb.sb("epsb", [128, 1], F32)
    scT = kb.sb("scT", [128, 8, 17], BF16)
    sc_p = kb.sb("sc_p", [128, 8, 128], BF16); sc_s = kb.sb("sc_s", [128, 8, 64], BF16)
    modT = kb.sb("modT", [128, 24, 17], F32)
    opsT = kb.sb("opsT", [128, 8, 17], F32)
    gate_p = kb.sb("gate_p", [128, D], F32); gate_s = kb.sb("gate_s", [64, D], F32)
    gpre_b = kb.sb("gpre_b", [128, D], F32); gpost_b = kb.sb("gpost_b", [128, D], F32)
    badab = kb.sb("badab", [1, 3 * D], BF16)
    pp = kb.sb("pp", [128, 4, 38], F32)
    gsub = kb.sb("gsub", [128, 1], F32)
    lq = kb.sb("lq", [128, 256], F32); lamt = kb.sb("lamt", [128, 4], F32)
    wpool = kb.sb("wpool", [128, 4, 128], BF16)
    idx = kb.sb("idx", [128, 256], I32)
    halo_u = kb.sb("halo_u", [128, 4, 15], F32); halo_p = kb.sb("halo_p", [128, 4, 2], F32)
    halo_g = kb.sb("halo_g", [128, 4, 30], F32)
    tok1 = kb.sb("tok1", [128, D], F32); tok2 = kb.sb("tok2", [128, D], BF16)
    tok3 = kb.sb("tok3", [128, 1536], F32)
    small = kb.sb("small", [128, 8], F32)
    stnew = kb.sb("stnew", [128, 3, 4, 64], F32)
    sttok = kb.sb("sttok", [128, BW], F32)
    AW = 17664
    arena = kb.sb("arena", [128, AW], F32)
    aptr = [0]

    def carve(name, shape, dt):
        n = int(np.prod(shape[1:]))
        nf = n if dt == F32 else (n + 1) // 2
        ap = arena[:, aptr[0]:aptr[0] + nf]
        aptr[0] += nf
        assert aptr[0] <= AW, (name, aptr[0])
        if dt != F32:
            ap = ap.bitcast(dt)
        ap = ap[:, 0:n]
        if len(shape) == 3:
            ap = ap.rearrange("p (a b) -> p a b", a=shape[1])
        elif len(shape) == 4:
            ap = ap.rearrange("p (a b c) -> p a b c", a=shape[1], b=shape[2])
        elif len(shape) == 5:
            ap = ap.rearrange("p (a b c d) -> p a b c d", a=shape[1], b=shape[2], c=shape[3])
        return kb.view(name, ap)

    pg = [kb.ps(f"pg{i}", [128, 512], F32) for i in range(2)]
    psc = [kb.ps(f"psc{i}", [128, 512], F32) for i in range(2)]
    pacc = [kb.ps(f"pacc{i}", [128, 512], F32) for i in range(4)]
    pgc = [0]; pscc = [0]

    def next_pg():
        pgc[0] += 1
        return pg[pgc[0] % 2]

    def next_psc():
        pscc[0] += 1
        return psc[pscc[0] % 2]

    def load(eng, dst, dst_ap, src_ap):
        return kb.dma(eng, [lambda h: h.dma_start(out=dst_ap, in_=src_ap)], writes=[dst])

    def wload(src3):
        w = wb[wctr[0] % NW]
        wctr[0] += 1
        kc, cols = src3.shape[1], src3.shape[2]
        kb.dma(POOL, [lambda h: h.dma_start(out=w[:, 0:kc, 0:cols], in_=src3)], writes=[w])
        return w

    def mm_group(out_ap, pairs, reads, wr):
        n = len(pairs)
        for i, (lt, rh) in enumerate(pairs):
            kb.op(PE, lambda h, lt=lt, rh=rh, i=i: h.matmul(out_ap, lt, rh, start=(i == 0), stop=(i == n - 1)),
                  reads=reads, writes=[wr], inc=(i == n - 1))

    def transpose_to(dst_ps_ap, src_ap, idt, reads, wr, inc=True):
        kb.op(PE, lambda h: h.transpose(dst_ps_ap, src_ap, idt), reads=reads, writes=[wr], inc=inc)

    def v3(ap2, nb):
        return ap2.rearrange("p (b t) -> p b t", b=nb)

    def cut(n, cond=True):
        if cond and _STOP[0] == n:
            raise _Stop()

    load(SP, ident, ident[:], c_ident)
    load(SP, trif, trif[:], c_tri)
    load(SP, cs, cs[:], c_cs.rearrange("a b p c -> p a b c"))
    load(SP, pinv, pinv[:].rearrange("p g t -> p (g t)"), c_pinv.rearrange("g t -> (g t)").partition_broadcast(128))
    kb.dma(POOL, [lambda h: h.dma_start(out=smask[:], in_=c_smask)], writes=[smask])
    kb.op(DVE, lambda h: h.tensor_copy(out=identb[:], in_=ident[:]), reads=[ident], writes=[identb])
    kb.op(DVE, lambda h: h.tensor_copy(out=trib[:], in_=trif[:]), reads=[trif], writes=[trib])
    kb.op(DVE, lambda h: h.memset(onesb[:], 1.0), writes=[onesb])
    kb.op(DVE, lambda h: h.memset(epsb[:], EPS), writes=[epsb])
    pti = tok3[:, 0:256].bitcast(I32)
    ptf = tok1[:, 0:256]
    offi = tok3[:, 256:257].bitcast(I32)
    offf = tok1[:, 256:257]
    load(SP, tok3, pti, ptab.rearrange("b j -> (b j)").partition_broadcast(128))
    kb.op(POOL, lambda h: h.iota(offi, pattern=[[0, 1]], base=0, channel_multiplier=1), reads=[], writes=[tok3])
    kb.op(DVE, lambda h: h.tensor_copy(out=offf, in_=offi), reads=[tok3], writes=[tok1])
    kb.op(DVE, lambda h: h.tensor_copy(out=ptf, in_=pti), reads=[tok3], writes=[tok1])
    kb.op(DVE, lambda h: h.tensor_scalar(out=ptf, in0=ptf, scalar1=128.0, scalar2=offf,
                                         op0=ALU.mult, op1=ALU.add), reads=[tok1], writes=[tok1])
    kb.op(DVE, lambda h: h.tensor_copy(out=idx[:], in_=ptf), reads=[tok1], writes=[idx])
    load(SP, tok1, tok1[0:17, :], c_all)
    kb.op(ACT, lambda h: h.activation(out=tok2[0:17, :], in_=tok1[0:17, :], func=AF.Silu), reads=[tok1], writes=[tok2])
    p = next_pg()
    pbf = p[:].bitcast(BF16)
    for kc in range(8):
        transpose_to(pbf[:, kc * 32:kc * 32 + 17], tok2[0:17, kc * 128:(kc + 1) * 128], identb[0:17, 0:17],
                     [tok2, identb], p, inc=(kc == 7))
    kb.op(DVE, lambda h: h.tensor_copy(out=scT[:], in_=pbf[:, 0:256].rearrange("p (k c) -> p k c", c=32)[:, :, 0:17]),
          reads=[p], writes=[scT])
    kb.op(DVE, lambda h: h.tensor_copy(out=sc_p[:], in_=scT[:, :, 0:1].to_broadcast([128, 8, 128])),
          reads=[scT], writes=[sc_p])
    kb.op(DVE, lambda h: h.tensor_copy(out=sc_s[:].rearrange("p k (b t) -> p k b t", t=4),
                                       in_=scT[:, :, 1:17].unsqueeze(3).to_broadcast([128, 8, 16, 4])),
          reads=[scT], writes=[sc_s])

    cut(1)
    for l in range(DEPTH):
        lam_init = LAM_INIT[l]
        load(SP, gpre_b, gpre_b[:], g_pre[l].partition_broadcast(128))
        load(SP, gpost_b, gpost_b[:], g_post[l].partition_broadcast(128))
        kb.dma(POOL, [lambda h: h.dma_start(out=badab[:], in_=b_ada[l:l + 1, :])], writes=[badab])
        load(SP, gsub, gsub[:], g_subln[l].rearrange("(p o) -> p o", o=1))
        load(SP, lq, lq[:], lam_qk[l].partition_broadcast(128))
        load(SP, tok1, tok1[0:38, 0:BW], ppack[l])
        p = next_pg()
        for c in range(4):
            transpose_to(p[:, c * 38:(c + 1) * 38], tok1[0:38, c * 128:(c + 1) * 128], ident[0:38, 0:38],
                         [tok1, ident], p, inc=(c == 3))
        kb.op(DVE, lambda h: h.tensor_copy(out=pp[:], in_=p[:, 0:152].rearrange("p (c r) -> p c r", r=38)),
              reads=[p], writes=[pp])
        kb.dma(POOL, [lambda h: h.dma_start(out=wpool[:], in_=w_pool[l].rearrange("g c e -> c g e"))], writes=[wpool])
        lq4 = lq[:].rearrange("p (a b d) -> p a b d", a=2, b=2)
        kb.op(DVE, lambda h: h.tensor_tensor(out=tok1[:, 0:128].rearrange("p (a d) -> p a d", a=2),
                                             in0=lq4[:, :, 0, :], in1=lq4[:, :, 1, :], op=ALU.mult), reads=[lq], writes=[tok1])
        kb.op(DVE, lambda h: h.tensor_reduce(out=lamt[:, 0:2], in_=tok1[:, 0:128].rearrange("p (a d) -> p a d", a=2),
                                             axis=AX.X, op=ALU.add), reads=[tok1], writes=[lamt])
        kb.op(ACT, lambda h: h.activation(out=lamt[:, 0:2], in_=lamt[:, 0:2], func=AF.Exp), reads=[lamt], writes=[lamt])
        kb.op(DVE, lambda h: h.tensor_tensor(out=lamt[:, 2:3], in0=lamt[:, 1:2], in1=lamt[:, 0:1], op=ALU.subtract),
              reads=[lamt], writes=[lamt])
        kb.op(DVE, lambda h: h.tensor_scalar(out=lamt[:, 2:3], in0=lamt[:, 2:3], scalar1=-lam_init, scalar2=None,
                                             op0=ALU.add), reads=[lamt], writes=[lamt])
        for ci in range(6):
            w = wload(w_ada[l][:, ci * 512:(ci + 1) * 512].rearrange("(kc p) c -> p kc c", p=128))
            p = next_pg()
            for j in range(4):
                m = ci * 4 + j
                pairs = [(w[:, kc, j * 128:(j + 1) * 128], scT[:, kc, :]) for kc in range(8)]
                pairs.append((badab[0:1, m * 128:(m + 1) * 128], onesb[0:1, 0:17]))
                mm_group(p[:, j * 32:j * 32 + 17], pairs, [w, scT, badab, onesb], p)
            kb.op(DVE, lambda h: h.tensor_copy(
                out=modT[:, ci * 4:(ci + 1) * 4, :], in_=p[:, 0:128].rearrange("p (j c) -> p j c", c=32)[:, :, 0:17]),
                reads=[p], writes=[modT])
            if ci >= 4:
                half = ci - 4
                p = next_pg()
                pairs = [(sc_p[:, kc, :], w[:, kc, :]) for kc in range(8)]
                pairs.append((onesb[0:1, 0:128], badab[0:1, ci * 512:(ci + 1) * 512]))
                mm_group(p[:, :], pairs, [w, sc_p, badab, onesb], p)
                kb.op(ACT, lambda h: h.copy(out=gate_p[:, half * 512:(half + 1) * 512], in_=p[:, :]),
                      reads=[p], writes=[gate_p])
                p = next_pg()
                pairs = [(sc_s[:, kc, :], w[:, kc, :]) for kc in range(8)]
                pairs.append((onesb[0:1, 0:64], badab[0:1, ci * 512:(ci + 1) * 512]))
                mm_group(p[0:64, :], pairs, [w, sc_s, badab, onesb], p)
                kb.op(ACT, lambda h: h.copy(out=gate_s[:, half * 512:(half + 1) * 512], in_=p[0:64, :]),
                      reads=[p], writes=[gate_s])
        kb.op(DVE, lambda h: h.tensor_scalar(out=opsT[:], in0=modT[:, 8:16, :], scalar1=1.0, scalar2=None, op0=ALU.add),
              reads=[modT], writes=[opsT])

        cut(2, l == 0)

        def win_chunk(ci):
            return w_in[l][:, ci * 512:(ci + 1) * 512].rearrange("(kc p) c -> p kc c", p=128)

        for ti in range(5):
            samp = (ti == 4)
            NT = 64 if samp else 512
            nblk = 1 if samp else 4
            rows = 64 if samp else 128
            nbv = NSB if samp else 1
            if ti == 0 or samp:
                kb.barrier()
                aptr[0] = 0
                hT = carve("hT", [128, 8, NT], BF16)
                merged = carve("merged", [128, 8, NT], F32)
                ys = carve("ys", [128, 4, NT], BF16)
                qT = carve("qT", [128, 4, NT], BF16)
                sg = carve("sg", [128, 4, NT], BF16)
                cc = carve("cc", [128, 4, NT], F32)
                cgs = carve("cgs", [128, 4, NT], F32)
                bgs = carve("bgs", [128, 4, NT], BF16)
                tA = carve("tA", [128, 576], F32); tB = carve("tB", [128, 576], F32)
                tC = carve("tC", [128, NT], F32); tD = carve("tD", [128, NT], F32); tE = carve("tE", [128, NT], F32)
                tb1 = carve("tb1", [128, 512], BF16); tb2 = carve("tb2", [128, 512], BF16)
                if samp:
                    stfm_u = carve("stfm_u", [128, 4, NSB * 15], F32)
                    stfm_p = carve("stfm_p", [128, 4, NSB * 2], F32)
                    stfm_g = carve("stfm_g", [128, 4, NSB * 30], F32)
                    kpg = [carve(f"kpg{i}", [128, 4, BW], BF16) for i in range(2)]
                    vpg = [carve(f"vpg{i}", [128, 4, BW], BF16) for i in range(2)]
                    kTs = carve("kTs", [128, 4, 4, 128], BF16)
                    Qall = carve("Qall", [128, 4, NSB, 2, 4], BF16)
                    pTs = carve("pTs", [128, 16, 32], BF16)
                    pTn = carve("pTn", [128, 32], BF16)
                    oS = carve("oS", [128, 4, 64], F32)
                    rinv = carve("rinv", [128, 32], F32)
                else:
                    stfm_u = stfm_p = stfm_g = None

            def ipf(w, j):
                pq = next_pg()
                mm_group(pq[:, 0:NT], [(w[:, kc, j * 128:(j + 1) * 128], hT[:, kc, 0:NT]) for kc in range(8)], [w, hT], pq)
                return pq

            if l == 0:
                if samp:
                    load(SP, xt, xt[0:64, 0, :], xs)
                else:
                    load(SP, xt, xt[:], xp[ti * 512:(ti + 1) * 512, :].rearrange("(j p) d -> p j d", p=128))
            else:
                if samp:
                    kb.dma(SP, [lambda h: h.dma_start(out=xt[0:64, 0, :], in_=x1[SEQ:SEQ + 64, :])], reads=[x1buf], writes=[xt])
                else:
                    kb.dma(SP, [lambda h: h.dma_start(out=xt[:], in_=x1[ti * 512:(ti + 1) * 512, :].rearrange("(j p) d -> p j d", p=128))],
                           reads=[x1buf], writes=[xt])

            for j in range(nblk):
                kb.op(DVE, lambda h: h.memset(small[:, 0:1], 0.0), writes=[small])
                kb.op(ACT, lambda h: h.activation(out=tok2[0:rows, :], in_=xt[0:rows, j, :], func=AF.Square,
                                                  accum_out=small[0:rows, 0:1]), reads=[xt, small], writes=[tok2, small])
                kb.op(ACT, lambda h: h.activation(out=small[0:rows, 1:2], in_=small[0:rows, 0:1], func=AF.Sqrt,
                                                  bias=epsb[0:rows, 0:1], scale=1.0 / D), reads=[small, epsb], writes=[small])
                kb.op(DVE, lambda h: h.reciprocal(out=small[0:rows, 2:3], in_=small[0:rows, 1:2]), reads=[small], writes=[small])
                kb.op(DVE, lambda h: h.scalar_tensor_tensor(out=tok2[0:rows, :], in0=xt[0:rows, j, :], scalar=small[0:rows, 2:3],
                                                            in1=gpre_b[0:rows, :], op0=ALU.mult, op1=ALU.mult),
                      reads=[xt, small, gpre_b], writes=[tok2])
                p = next_pg()
                pbf = p[:].bitcast(BF16)
                for kc in range(8):
                    transpose_to(pbf[:, kc * 128:kc * 128 + rows], tok2[0:rows, kc * 128:(kc + 1) * 128],
                                 identb[0:rows, 0:rows], [tok2, identb], p, inc=(kc == 7))
                src = pbf[:, :].rearrange("p (k t) -> p k t", t=128)[:, :, 0:rows]
                if not samp:
                    t38 = tok3[:, 0:1024].rearrange("p (k t) -> p k t", t=128)
                    kb.op(DVE, lambda h: h.tensor_tensor(out=t38, in0=src, in1=opsT[:, :, 0:1].to_broadcast([128, 8, 128]), op=ALU.mult),
                          reads=[p, opsT], writes=[tok3])
                    kb.op(DVE, lambda h: h.tensor_tensor(out=hT[:, :, j * 128:(j + 1) * 128], in0=t38,
                                                         in1=modT[:, 0:8, 0:1].to_broadcast([128, 8, 128]), op=ALU.add),
                          reads=[tok3, modT], writes=[hT])
                else:
                    t38 = tok3[:, 0:512].rearrange("p (k b t) -> p k b t", k=8, t=4)
                    kb.op(DVE, lambda h: h.tensor_tensor(out=t38, in0=src.rearrange("p k (b t) -> p k b t", t=4),
                                                         in1=opsT[:, :, 1:17].unsqueeze(3).to_broadcast([128, 8, 16, 4]), op=ALU.mult),
                          reads=[p, opsT], writes=[tok3])
                    kb.op(DVE, lambda h: h.tensor_tensor(out=hT[:, :, 0:64].rearrange("p k (b t) -> p k b t", t=4), in0=t38,
                                                         in1=modT[:, 0:8, 1:17].unsqueeze(3).to_broadcast([128, 8, 16, 4]), op=ALU.add),
                          reads=[tok3, modT], writes=[hT])

            cut(31, l == 0 and ti == 0)
            wq = wload(win_chunk(6)); wk = wload(win_chunk(7))
            for j in range(nblk):
                xb = ti * 4 + j
                for which, w in ((0, wq), (1, wk)):
                    p = next_pg()
                    mm_group(p[0:rows, :], [(hT[:, kc, j * 128:j * 128 + rows], w[:, kc, :]) for kc in range(8)], [hT, w], p)
                    cosb = cs[0:rows, 0, xb, :].unsqueeze(1).to_broadcast([rows, 8, 32])
                    sinb = cs[0:rows, 1, xb, :].unsqueeze(1).to_broadcast([rows, 8, 32])
                    o3 = tok3[0:rows, 0:512].rearrange("p (g two d) -> p g two d", two=2, d=32)
                    t3 = tok3[0:rows, 512:1024].rearrange("p (g two d) -> p g two d", two=2, d=32)
                    kb.op(ACT, lambda h: h.copy(out=tok1[0:rows, 0:512], in_=p[0:rows, :]), reads=[p], writes=[tok1])
                    s3 = tok1[0:rows, 0:512].rearrange("p (g two d) -> p g two d", two=2, d=32)
                    cut(33, l == 0 and ti == 0)
                    for half in range(2):
                        kb.op(DVE, lambda h: h.tensor_tensor(out=o3[:, :, half, :], in0=s3[:, :, half, :], in1=cosb, op=ALU.mult),
                              reads=[tok1, cs], writes=[tok3])
                        kb.op(DVE, lambda h: h.tensor_tensor(out=t3[:, :, half, :], in0=s3[:, :, 1 - half, :], in1=sinb, op=ALU.mult),
                              reads=[tok1, cs], writes=[tok3])
                    kb.op(DVE, lambda h: h.tensor_tensor(out=o3[:, :, 0, :], in0=o3[:, :, 0, :], in1=t3[:, :, 0, :], op=ALU.subtract),
                          reads=[tok3], writes=[tok3])
                    kb.op(DVE, lambda h: h.tensor_tensor(out=o3[:, :, 1, :], in0=o3[:, :, 1, :], in1=t3[:, :, 1, :], op=ALU.add),
                          reads=[tok3], writes=[tok3])
                    cut(34, l == 0 and ti == 0)
                    kb.op(ACT, lambda h: h.copy(out=tok2[0:rows, 0:512], in_=tok3[0:rows, 0:512]), reads=[tok3], writes=[tok2])
                    if which == 1:
                        dst = k_s[l] if samp else k_p[l][xb * 128:(xb + 1) * 128, :]
                        kb.dma(SP, [lambda h: h.dma_start(out=dst, in_=tok3[0:rows, 0:512])], reads=[tok3], store=True)
                    pt = next_pg()
                    ptb = pt[:].bitcast(BF16)
                    for hh in range(4):
                        transpose_to(ptb[:, hh * 128:hh * 128 + rows], tok2[0:rows, hh * 128:(hh + 1) * 128],
                                     identb[0:rows, 0:rows], [tok2, identb], pt, inc=(hh == 3))
                    srcT = ptb[:, 0:512].rearrange("p (h t) -> p h t", t=128)[:, :, 0:rows]
                    if which == 0:
                        kb.op(ACT, lambda h: h.copy(out=qT[:, :, j * 128:j * 128 + rows], in_=srcT), reads=[pt], writes=[qT])
                    else:
                        kb.op(ACT, lambda h: h.copy(out=kT[:, :, xb * 128:xb * 128 + rows], in_=srcT), reads=[pt], writes=[kT])
                    cut(35, l == 0 and ti == 0 and which == 0)
                    cut(36, l == 0 and ti == 0 and which == 1)
            cut(32, l == 0 and ti == 0)
            wv = wload(win_chunk(8))
            for j in range(nblk):
                xb = ti * 4 + j
                p = next_pg()
                mm_group(p[0:rows, :], [(hT[:, kc, j * 128:j * 128 + rows], wv[:, kc, :]) for kc in range(8)], [hT, wv], p)
                kb.op(ACT, lambda h: h.copy(out=tok3[0:rows, 1024:1536], in_=p[0:rows, :]), reads=[p], writes=[tok3])
                kb.op(DVE, lambda h: h.tensor_copy(out=vtok[0:rows, xb, :], in_=p[0:rows, :]), reads=[p], writes=[vtok])
                dst = v_s[l] if samp else v_p[l][xb * 128:(xb + 1) * 128, :]
                kb.dma(SP, [lambda h: h.dma_start(out=dst, in_=tok3[0:rows, 1024:1536])], reads=[tok3], store=True)

            cut(3, l == 0 and ti == 0)
            cut(13, l == 0 and ti == 4)

            def emit_rows(get_ap, n, dsts):
                pe_ = next_pg()
                for c in range(4):
                    transpose_to(pe_[0:n, c * 128:(c + 1) * 128], get_ap(c), ident[:, :], [stnew, ident], pe_, inc=(c == 3))
                kb.op(ACT, lambda h: h.copy(out=sttok[0:n, :], in_=pe_[0:n, :]), reads=[pe_], writes=[sttok])
                kb.dma(SP, [(lambda h, r0=r0, nr=nr, d_=d_: h.dma_start(out=d_, in_=sttok[r0:r0 + nr, :])) for (r0, nr, d_) in dsts],
                       reads=[sttok], store=True)

            def branch_merge(n):
                for half in range(2):
                    wm = wload(win_chunk(13 + 2 * n + half))
                    wbr = wload(w_branch[l, n][:, half * 512:(half + 1) * 512].rearrange("(kc p) c -> p kc c", p=128))
                    for jj in range(4):
                        dc = half * 4 + jj
                        pm = ipf(wm, jj)
                        kb.op(ACT, lambda h: h.activation(out=tC[:, 0:NT], in_=pm[:, 0:NT], func=AF.Sigmoid), reads=[pm], writes=[tC])
                        p2 = next_pg()
                        mm_group(p2[:, 0:NT], [(wbr[:, kc, jj * 128:(jj + 1) * 128], ys[:, kc, 0:NT]) for kc in range(4)], [wbr, ys], p2)
                        if n == 0:
                            kb.op(DVE, lambda h: h.tensor_tensor(out=merged[:, dc, 0:NT], in0=p2[:, 0:NT], in1=tC[:, 0:NT], op=ALU.mult),
                                  reads=[p2, tC], writes=[merged])
                        else:
                            kb.op(DVE, lambda h: h.tensor_tensor(out=tD[:, 0:NT], in0=p2[:, 0:NT], in1=tC[:, 0:NT], op=ALU.mult),
                                  reads=[p2, tC], writes=[tD])
                            kb.op(DVE, lambda h: h.tensor_tensor(out=merged[:, dc, 0:NT], in0=merged[:, dc, 0:NT], in1=tD[:, 0:NT], op=ALU.add),
                                  reads=[merged, tD], writes=[merged])

            def silu_gates(n):
                wg = wload(win_chunk(9 + n))
                for c in range(4):
                    pq = ipf(wg, c)
                    kb.op(ACT, lambda h: h.activation(out=sg[:, c, 0:NT], in_=pq[:, 0:NT], func=AF.Silu), reads=[pq], writes=[sg])

            def load_state_fm(src2d, nrows, dst):
                r0 = 0
                while r0 < nrows:
                    nr = min(128, nrows - r0)
                    load(SP, tok1, tok1[0:nr, 0:BW], src2d[r0:r0 + nr, :])
                    pq = next_pg()
                    for c in range(4):
                        transpose_to(pq[:, c * 128:c * 128 + nr], tok1[0:nr, c * 128:(c + 1) * 128], ident[0:nr, 0:nr],
                                     [tok1, ident], pq, inc=(c == 3))
                    kb.op(ACT, lambda h: h.copy(out=dst[:, :, r0:r0 + nr],
                                                in_=pq[:, :].rearrange("p (c t) -> p c t", t=128)[:, :, 0:nr]),
                          reads=[pq], writes=[dst])
                    r0 += nr

            def fill_seq(buf, g, npre, halo, stfm, src_fn):
                if samp:
                    seq = buf[:, 0:NSB * (npre + 4)].rearrange("p (b t) -> p b t", t=npre + 4)
                    L = 4
                    kb.op(DVE, lambda h: h.tensor_copy(out=seq[:, :, 0:npre],
                                                       in_=stfm[:, g, :].rearrange("p (b r) -> p b r", r=npre)),
                          reads=[stfm], writes=[buf])
                else:
                    seq = buf[:, 0:npre + NT].rearrange("p (b t) -> p b t", b=1)
                    L = NT
                    if ti == 0:
                        kb.op(DVE, lambda h: h.memset(seq[:, :, 0:npre], 0.0), writes=[buf])
                    else:
                        kb.op(DVE, lambda h: h.tensor_copy(out=seq[:, 0, 0:npre], in_=halo[:, g, :]), reads=[halo], writes=[buf])
                src_fn(seq[:, :, npre:npre + L])
                if not samp:
                    kb.op(DVE, lambda h: h.tensor_copy(out=halo[:, g, :], in_=seq[:, 0, L:L + npre]), reads=[buf], writes=[halo])
                return seq, L

            if samp:
                load_state_fm(st_pool[l].rearrange("b r c -> (b r) c"), NSB * 15, stfm_u)
                load_state_fm(st_sconv[l].rearrange("b r c -> (b r) c"), NSB * 2, stfm_p)
                load_state_fm(st_cconv[l].rearrange("b r c -> (b r) c"), NSB * 30, stfm_g)
                dd = Buf(f"stcopy{l}")
                kb.dma(SP, [lambda h: h.dma_start(out=pool_s[l][:, 0:11, :], in_=st_pool[l][:, 4:15, :]),
                            lambda h: h.dma_start(out=cconv_s[l][:, 0:26, :], in_=st_cconv[l][:, 4:30, :])],
                       sem_of=dd, store=True)
            want_state = samp or ti == 3

            silu_gates(0)
            wu = wload(win_chunk(0))
            for g in range(4):
                win = WINS[g]
                p = ipf(wu, g)
                seq, L = fill_seq(tA, g, 15, halo_u, stfm_u,
                                  lambda o: kb.op(ACT, lambda h: h.copy(out=o, in_=v3(p[:, 0:NT], nbv)), reads=[p], writes=[tA]))
                if want_state:
                    ncol = 64 if samp else 15
                    c0 = 0 if samp else NT - 15
                    kb.op(ACT, lambda h: h.copy(out=stnew[:, 0, g, 0:ncol], in_=p[:, c0:c0 + ncol]), reads=[p], writes=[stnew])
                acc = v3(tB[:, 0:NT], nbv)
                kb.op(DVE, lambda h: h.tensor_tensor(out=acc, in0=seq[:, :, 15:15 + L], in1=seq[:, :, 14:14 + L], op=ALU.add),
                      reads=[tA], writes=[tB])
                for i in range(2, win):
                    kb.op(DVE, lambda h: h.tensor_tensor(out=acc, in0=acc, in1=seq[:, :, 15 - i:15 - i + L], op=ALU.add),
                          reads=[tA, tB], writes=[tB])
                kb.op(DVE, lambda h: h.scalar_tensor_tensor(out=v3(tb1[:, 0:NT], nbv), in0=acc, scalar=1.0 / win,
                                                            in1=seq[:, :, 15:15 + L], op0=ALU.mult, op1=ALU.subtract),
                      reads=[tA, tB], writes=[tb1])
                if ti == 0:
                    kb.op(DVE, lambda h: h.tensor_tensor(out=tC[:, 0:16], in0=tB[:, 0:16], in1=pinv[:, g, :], op=ALU.mult),
                          reads=[tB, pinv], writes=[tC])
                    kb.op(DVE, lambda h: h.tensor_tensor(out=tb1[:, 0:16], in0=tC[:, 0:16], in1=tA[:, 15:31], op=ALU.subtract),
                          reads=[tC, tA], writes=[tb1])
                p2 = next_pg()
                mm_group(p2[:, 0:NT], [(wpool[:, g, :], tb1[:, 0:NT])], [wpool, tb1], p2)
                kb.op(DVE, lambda h: h.scalar_tensor_tensor(out=ys[:, g, 0:NT], in0=p2[:, 0:NT], scalar=pp[:, g, 0:1],
                                                            in1=sg[:, g, 0:NT], op0=ALU.mult, op1=ALU.mult),
                      reads=[p2, pp, sg], writes=[ys])
            branch_merge(0)

            cut(4, l == 0 and ti == 0)
            cut(14, l == 0 and ti == 4)
            silu_gates(1)
            wbg = wload(win_chunk(1))
            for c in range(4):
                p = ipf(wbg, c)
                kb.op(ACT, lambda h: h.copy(out=bgs[:, c, 0:NT], in_=p[:, 0:NT]), reads=[p], writes=[bgs])
            wcg = wload(win_chunk(2))
            for c in range(4):
                p = ipf(wcg, c)
                kb.op(ACT, lambda h: h.copy(out=cgs[:, c, 0:NT], in_=p[:, 0:NT]), reads=[p], writes=[cgs])
            whb = wload(win_chunk(3))
            for c in range(4):
                p = ipf(whb, c)
                seq, L = fill_seq(tA, c, 2, halo_p, stfm_p,
                                  lambda o: kb.op(DVE, lambda h: h.tensor_tensor(out=o, in0=v3(p[:, 0:NT], nbv), in1=v3(cgs[:, c, 0:NT], nbv), op=ALU.mult),
                                                  reads=[p, cgs], writes=[tA]))
                if want_state:
                    if samp:
                        kb.op(DVE, lambda h: h.tensor_copy(out=stnew[:, 1, c, 0:64].rearrange("p (b t) -> p b t", t=4), in_=seq[:, :, 2:6]),
                              reads=[tA], writes=[stnew])
                    else:
                        kb.op(DVE, lambda h: h.tensor_copy(out=stnew[:, 1, c, 0:2], in_=seq[:, 0, NT:NT + 2]), reads=[tA], writes=[stnew])
                acc = v3(tB[:, 0:NT], nbv)
                kb.op(DVE, lambda h: h.tensor_scalar(out=acc, in0=seq[:, :, 0:L], scalar1=pp[:, c, 1:2], scalar2=None, op0=ALU.mult),
                      reads=[tA, pp], writes=[tB])
                for j in (1, 2):
                    kb.op(DVE, lambda h: h.scalar_tensor_tensor(out=acc, in0=seq[:, :, j:j + L], scalar=pp[:, c, 1 + j:2 + j],
                                                                in1=acc, op0=ALU.mult, op1=ALU.add),
                          reads=[tA, tB, pp], writes=[tB])
                kb.op(DVE, lambda h: h.tensor_tensor(out=tC[:, 0:NT], in0=tB[:, 0:NT], in1=bgs[:, c, 0:NT], op=ALU.mult),
                      reads=[tB, bgs], writes=[tC])
                kb.op(DVE, lambda h: h.tensor_tensor(out=ys[:, c, 0:NT], in0=tC[:, 0:NT], in1=sg[:, c, 0:NT], op=ALU.mult),
                      reads=[tC, sg], writes=[ys])
            branch_merge(1)

            cut(5, l == 0 and ti == 0)
            cut(15, l == 0 and ti == 4)
            silu_gates(2)
            wval = wload(win_chunk(4))
            for c in range(4):
                p = ipf(wval, c)
                kb.op(ACT, lambda h: h.copy(out=cgs[:, c, 0:NT], in_=p[:, 0:NT]), reads=[p], writes=[cgs])
            wgl = wload(win_chunk(5))
            for c in range(4):
                p = ipf(wgl, c)
                kb.op(ACT, lambda h: h.activation(out=tC[:, 0:NT], in_=p[:, 0:NT], func=AF.Sigmoid), reads=[p], writes=[tC])
                seq, L = fill_seq(tA, c, 30, halo_g, stfm_g,
                                  lambda o: kb.op(DVE, lambda h: h.tensor_tensor(out=o, in0=v3(cgs[:, c, 0:NT], nbv), in1=v3(tC[:, 0:NT], nbv), op=ALU.mult),
                                                  reads=[cgs, tC], writes=[tA]))
                if want_state:
                    if samp:
                        kb.op(DVE, lambda h: h.tensor_copy(out=stnew[:, 2, c, 0:64].rearrange("p (b t) -> p b t", t=4), in_=seq[:, :, 30:34]),
                              reads=[tA], writes=[stnew])
                    else:
                        kb.op(DVE, lambda h: h.tensor_copy(out=stnew[:, 2, c, 0:30], in_=seq[:, 0, NT:NT + 30]), reads=[tA], writes=[stnew])
                acc = v3(cc[:, c, 0:NT], nbv)
                kb.op(DVE, lambda h: h.tensor_scalar(out=acc, in0=seq[:, :, 0:L], scalar1=pp[:, c, 4:5], scalar2=pp[:, c, 35:36],
                                                     op0=ALU.mult, op1=ALU.add), reads=[tA, pp], writes=[cc])
                for j in range(1, 31):
                    kb.op(DVE, lambda h: h.scalar_tensor_tensor(out=acc, in0=seq[:, :, j:j + L], scalar=pp[:, c, 4 + j:5 + j],
                                                                in1=acc, op0=ALU.mult, op1=ALU.add),
                          reads=[tA, cc, pp], writes=[cc])
            for c in range(4):
                kb.op(ACT, lambda h: h.copy(out=bgs[:, c, 0:NT], in_=cc[:, c, 0:NT]), reads=[cc], writes=[bgs])
            ps1 = next_pg()
            mm_group(ps1[:, 0:NT], [(onesb[:, :], bgs[:, c, 0:NT]) for c in range(4)], [onesb, bgs], ps1)
            kb.op(ACT, lambda h: h.activation(out=tD[:, 0:NT], in_=ps1[:, 0:NT], func=AF.Copy, scale=1.0 / BW), reads=[ps1], writes=[tD])
            for c in range(4):
                kb.op(DVE, lambda h: h.tensor_tensor(out=cc[:, c, 0:NT], in0=cc[:, c, 0:NT], in1=tD[:, 0:NT], op=ALU.subtract),
                      reads=[cc, tD], writes=[cc])
                kb.op(ACT, lambda h: h.activation(out=bgs[:, c, 0:NT], in_=cc[:, c, 0:NT], func=AF.Square), reads=[cc], writes=[bgs])
            ps2 = next_pg()
            mm_group(ps2[:, 0:NT], [(onesb[:, :], bgs[:, c, 0:NT]) for c in range(4)], [onesb, bgs], ps2)
            kb.op(ACT, lambda h: h.activation(out=tD[:, 0:NT], in_=ps2[:, 0:NT], func=AF.Sqrt, bias=epsb[:, 0:1], scale=1.0 / BW),
                  reads=[ps2, epsb], writes=[tD])
            kb.op(DVE, lambda h: h.reciprocal(out=tD[:, 0:NT], in_=tD[:, 0:NT]), reads=[tD], writes=[tD])
            for c in range(4):
                kb.op(DVE, lambda h: h.tensor_tensor(out=tC[:, 0:NT], in0=cc[:, c, 0:NT], in1=tD[:, 0:NT], op=ALU.mult),
                      reads=[cc, tD], writes=[tC])
                kb.op(ACT, lambda h: h.activation(out=tE[:, 0:NT], in_=tC[:, 0:NT], func=AF.Silu, bias=pp[:, c, 37:38], scale=pp[:, c, 36:37]),
                      reads=[tC, pp], writes=[tE])
                kb.op(DVE, lambda h: h.tensor_tensor(out=ys[:, c, 0:NT], in0=tE[:, 0:NT], in1=sg[:, c, 0:NT], op=ALU.mult),
                      reads=[tE, sg], writes=[ys])
            branch_merge(2)

            if want_state:
                for si, (npre, dp, ds_) in enumerate(((15, pool_p, pool_s), (2, sconv_p, sconv_s), (30, cconv_p, cconv_s))):
                    if samp:
                        keep = min(4, npre)
                        dsts = [(b * 4 + (4 - keep), keep, ds_[l, b, npre - keep:npre, :]) for b in range(NSB)]
                        emit_rows(lambda c: stnew[:, si, c, 0:64], 64, dsts)
                    else:
                        emit_rows(lambda c: stnew[:, si, c, 0:npre], npre, [(0, npre, dp[l])])

            cut(6, l == 0 and ti == 0)
            cut(16, l == 0 and ti == 4)
            silu_gates(3)
            nlam = lamt[:, 2:3]
            O1, O2, R1, R2 = pacc
            if samp:
                kb.op(DVE, lambda h: h.memset(Qall[:], 0.0), writes=[Qall])
                for jm in range(2):
                    pr = slice(64 * jm, 64 * jm + 64)
                    kb.op(DVE, lambda h: h.tensor_copy(out=Qall[pr, :, :, jm, :],
                                                       in_=qT[pr, :, 0:64].rearrange("p h (b t) -> p h b t", t=4)),
                          reads=[qT], writes=[Qall])
                for b in range(NSB):
                    S_ps = next_psc()
                    for qtr in range(4):
                        hb = (b * 4 + qtr) % 2
                        kb.dma(POOL, [(lambda h, pgi=pgi: h.indirect_dma_start(
                            out=kpg[hb][:, pgi, :], out_offset=None, in_=ck[l],
                            in_offset=bass.IndirectOffsetOnAxis(ap=idx[:, b * 16 + qtr * 4 + pgi:b * 16 + qtr * 4 + pgi + 1], axis=0),
                            bounds_check=breg, oob_is_err=False)) for pgi in range(4)],
                            reads=[idx], writes=[kpg[hb]])
                        kb.dma(POOL, [(lambda h, pgi=pgi: h.indirect_dma_start(
                            out=vpg[hb][:, pgi, :], out_offset=None, in_=cv[l],
                            in_offset=bass.IndirectOffsetOnAxis(ap=idx[:, b * 16 + qtr * 4 + pgi:b * 16 + qtr * 4 + pgi + 1], axis=0),
                            bounds_check=breg, oob_is_err=False)) for pgi in range(4)],
                            reads=[idx], writes=[vpg[hb]])
                        for pgi in range(4):
                            if pgi % 2 == 0:
                                ptp = next_pg()
                                ptpb = ptp[:].bitcast(BF16)
                            for hh in range(4):
                                col = (pgi % 2) * 512 + hh * 128
                                transpose_to(ptpb[:, col:col + 128], kpg[hb][:, pgi, hh * 128:(hh + 1) * 128], identb[:, :],
                                             [kpg[hb], identb], ptp, inc=(pgi % 2 == 1 and hh == 3))
                            if pgi % 2 == 1:
                                kb.op(ACT, lambda h: h.copy(out=kTs[:, pgi - 1:pgi + 1, :, :],
                                                            in_=ptpb[:, 0:1024].rearrange("p (a h k) -> p a h k", a=2, h=4)),
                                      reads=[ptp], writes=[kTs])
                        for pgi in range(4):
                            page = qtr * 4 + pgi
                            for hh in range(4):
                                last = (pgi == 3 and hh == 3)
                                kb.op(PE, lambda h: h.matmul(S_ps[:, page * 32 + hh * 8:page * 32 + hh * 8 + 8], kTs[:, pgi, hh, :],
                                                             Qall[:, hh, b, :, :].rearrange("p j t -> p (j t)"), start=True, stop=True),
                                      reads=[kTs, Qall], writes=[S_ps], inc=last)
                        kb.op(ACT, lambda h: h.activation(out=pTs[:, qtr * 4:qtr * 4 + 4, :],
                                                          in_=S_ps[:, qtr * 128:(qtr + 1) * 128].rearrange("p (a c) -> p a c", c=32),
                                                          func=AF.Exp, scale=0.125), reads=[S_ps], writes=[pTs])
                        for pgi in range(4):
                            page = qtr * 4 + pgi
                            for hh in range(4):
                                kb.op(PE, lambda h: h.matmul(O1[:, hh * 8:hh * 8 + 8], vpg[hb][:, pgi, hh * 128:(hh + 1) * 128],
                                                             pTs[:, page, hh * 8:hh * 8 + 8], start=(page == 0 and hh == 0), stop=False),
                                      reads=[vpg[hb], pTs], writes=[O1], inc=False)
                            kb.op(PE, lambda h: h.matmul(R1[:, 0:32], onesb[:, :], pTs[:, page, :], start=(page == 0), stop=False),
                                  reads=[onesb, pTs], writes=[R1], inc=True)
                    S2 = next_psc()
                    for hh in range(4):
                        kb.op(PE, lambda h: h.matmul(S2[0:64, hh * 8:hh * 8 + 8], kT[:, hh, SEQ:SEQ + 64], Qall[:, hh, b, :, :].rearrange("p j t -> p (j t)"),
                                                     start=True, stop=True), reads=[kT, Qall], writes=[S2], inc=(hh == 3))
                    kb.op(ACT, lambda h: h.activation(out=pTn[0:64, :], in_=S2[0:64, 0:32], func=AF.Exp, scale=0.125), reads=[S2], writes=[pTn])
                    kb.op(DVE, lambda h: h.tensor_tensor(out=pTn[0:64, :].rearrange("p (h c) -> p h c", h=4), in0=pTn[0:64, :].rearrange("p (h c) -> p h c", h=4),
                                                         in1=smask[:, b * 8:(b + 1) * 8].unsqueeze(1).to_broadcast([64, 4, 8]), op=ALU.mult),
                          reads=[pTn, smask], writes=[pTn])
                    for hh in range(4):
                        kb.op(PE, lambda h: h.matmul(O1[:, hh * 8:hh * 8 + 8], vtok[0:64, 16, hh * 128:(hh + 1) * 128],
                                                     pTn[0:64, hh * 8:hh * 8 + 8], start=False, stop=True),
                              reads=[vtok, pTn], writes=[O1], inc=False)
                    kb.op(PE, lambda h: h.matmul(R1[:, 0:32], onesb[0:64, :], pTn[0:64, :], start=False, stop=True),
                          reads=[onesb, pTn], writes=[R1], inc=True)
                    kb.op(DVE, lambda h: h.reciprocal(out=rinv[:, :], in_=R1[:, 0:32]), reads=[R1], writes=[rinv])
                    kb.op(DVE, lambda h: h.tensor_tensor(out=rinv[:, :], in0=O1[:, 0:32], in1=rinv[:, :], op=ALU.mult), reads=[O1, rinv], writes=[rinv])
                    r4 = rinv[:, :].rearrange("p (h j t) -> p h j t", h=4, j=2)
                    kb.op(DVE, lambda h: h.scalar_tensor_tensor(out=oS[:, :, b * 4:(b + 1) * 4], in0=r4[:, :, 1, :], scalar=nlam, in1=r4[:, :, 0, :],
                                                                op0=ALU.mult, op1=ALU.add), reads=[rinv, lamt], writes=[oS])
            for hh in range(4):
                if not samp:
                    nkb = 4 * ti + 4
                    for kbi in range(nkb):
                        jd = kbi - 4 * ti
                        q0 = max(0, jd) * 128
                        for mp in range(2):
                            s_ps = next_psc()
                            pr = slice(64 * mp, 64 * mp + 64)
                            mm_group(s_ps[:, q0:NT], [(kT[pr, hh, kbi * 128:(kbi + 1) * 128], qT[pr, hh, q0:NT])], [kT, qT], s_ps)
                            pt_ = tb1 if mp == 0 else tb2
                            kb.op(ACT, lambda h: h.activation(out=pt_[:, q0:NT], in_=s_ps[:, q0:NT], func=AF.Exp, scale=0.125),
                                  reads=[s_ps], writes=[pt_])
                            if jd >= 0:
                                kb.op(DVE, lambda h: h.tensor_tensor(out=pt_[:, q0:q0 + 128], in0=pt_[:, q0:q0 + 128], in1=trib[:, :], op=ALU.mult),
                                      reads=[pt_, trib], writes=[pt_])
                            Oa, Ra = (O1, R1) if mp == 0 else (O2, R2)
                            kb.op(PE, lambda h: h.matmul(Oa[:, q0:NT], vtok[:, kbi, hh * 128:(hh + 1) * 128], pt_[:, q0:NT],
                                                         start=(kbi == 0), stop=(kbi == nkb - 1)), reads=[vtok, pt_], writes=[Oa], inc=False)
                            kb.op(PE, lambda h: h.matmul(Ra[:, q0:NT], onesb[:, :], pt_[:, q0:NT],
                                                         start=(kbi == 0), stop=(kbi == nkb - 1)), reads=[onesb, pt_], writes=[Ra], inc=True)
                    kb.op(DVE, lambda h: h.reciprocal(out=tC[:, 0:NT], in_=R1[:, 0:NT]), reads=[R1], writes=[tC])
                    kb.op(DVE, lambda h: h.reciprocal(out=tD[:, 0:NT], in_=R2[:, 0:NT]), reads=[R2], writes=[tD])
                    kb.op(DVE, lambda h: h.tensor_tensor(out=tC[:, 0:NT], in0=O1[:, 0:NT], in1=tC[:, 0:NT], op=ALU.mult), reads=[O1, tC], writes=[tC])
                    kb.op(DVE, lambda h: h.tensor_tensor(out=tD[:, 0:NT], in0=O2[:, 0:NT], in1=tD[:, 0:NT], op=ALU.mult), reads=[O2, tD], writes=[tD])
                    kb.op(DVE, lambda h: h.scalar_tensor_tensor(out=tE[:, 0:NT], in0=tD[:, 0:NT], scalar=nlam, in1=tC[:, 0:NT],
                                                                op0=ALU.mult, op1=ALU.add), reads=[tD, tC, lamt], writes=[tE])
                else:
                    kb.op(DVE, lambda h: h.tensor_copy(out=tE[:, 0:NT], in_=oS[:, hh, :]), reads=[oS], writes=[tE])
                kb.op(ACT, lambda h: h.activation(out=tb1[:, 0:NT], in_=tE[:, 0:NT], func=AF.Square), reads=[tE], writes=[tb1])
                pss = next_pg()
                mm_group(pss[:, 0:NT], [(onesb[:, :], tb1[:, 0:NT])], [onesb, tb1], pss)
                kb.op(ACT, lambda h: h.activation(out=tC[:, 0:NT], in_=pss[:, 0:NT], func=AF.Sqrt, bias=epsb[:, 0:1], scale=1.0 / 128),
                      reads=[pss, epsb], writes=[tC])
                kb.op(DVE, lambda h: h.reciprocal(out=tC[:, 0:NT], in_=tC[:, 0:NT]), reads=[tC], writes=[tC])
                kb.op(DVE, lambda h: h.scalar_tensor_tensor(out=tD[:, 0:NT], in0=tE[:, 0:NT], scalar=gsub[:, 0:1], in1=tC[:, 0:NT],
                                                            op0=ALU.mult, op1=ALU.mult), reads=[tE, gsub, tC], writes=[tD])
                kb.op(DVE, lambda h: h.scalar_tensor_tensor(out=ys[:, hh, 0:NT], in0=tD[:, 0:NT], scalar=1.0 - lam_init, in1=sg[:, hh, 0:NT],
                                                            op0=ALU.mult, op1=ALU.mult), reads=[tD, sg], writes=[ys])
            branch_merge(3)

            cut(7, l == 0 and ti == 0)
            cut(17, l == 0 and ti == 4)
            for c in range(8):
                kb.op(ACT, lambda h: h.copy(out=hT[:, c, 0:NT], in_=merged[:, c, 0:NT]), reads=[merged], writes=[hT])
            wo = [wload(w_o[l][:, hf * 512:(hf + 1) * 512].rearrange("(kc p) c -> p kc c", p=128)) for hf in range(2)]
            for j in range(nblk):
                for hf in range(2):
                    p = next_pg()
                    mm_group(p[0:rows, :], [(hT[:, kc, j * 128:j * 128 + rows], wo[hf][:, kc, :]) for kc in range(8)], [hT, wo[hf]], p)
                    kb.op(ACT, lambda h: h.copy(out=tok1[0:rows, hf * 512:(hf + 1) * 512], in_=p[0:rows, :]), reads=[p], writes=[tok1])
                kb.op(DVE, lambda h: h.memset(small[:, 4:5], 0.0), writes=[small])
                kb.op(ACT, lambda h: h.activation(out=tok2[0:rows, :], in_=tok1[0:rows, :], func=AF.Square, accum_out=small[0:rows, 4:5]),
                      reads=[tok1, small], writes=[tok2, small])
                kb.op(ACT, lambda h: h.activation(out=small[0:rows, 5:6], in_=small[0:rows, 4:5], func=AF.Sqrt, bias=epsb[0:rows, 0:1], scale=1.0 / D),
                      reads=[small, epsb], writes=[small])
                kb.op(DVE, lambda h: h.reciprocal(out=small[0:rows, 6:7], in_=small[0:rows, 5:6]), reads=[small], writes=[small])
                kb.op(DVE, lambda h: h.scalar_tensor_tensor(out=tok1[0:rows, :], in0=tok1[0:rows, :], scalar=small[0:rows, 6:7], in1=gpost_b[0:rows, :],
                                                            op0=ALU.mult, op1=ALU.mult), reads=[tok1, small, gpost_b], writes=[tok1])
                gt = gate_s if samp else gate_p
                kb.op(DVE, lambda h: h.tensor_tensor(out=tok1[0:rows, :], in0=tok1[0:rows, :], in1=gt[0:rows, :], op=ALU.mult),
                      reads=[tok1, gt], writes=[tok1])
                kb.op(DVE, lambda h: h.tensor_tensor(out=xt[0:rows, j, :], in0=xt[0:rows, j, :], in1=tok1[0:rows, :], op=ALU.add),
                      reads=[xt, tok1], writes=[xt])
            cut(8, l == 0 and ti == 0)
            cut(9, l == 0 and ti == 3)
            cut(10, l == 0 and ti == 4)
            if l == DEPTH - 1:
                if samp:
                    kb.dma(SP, [lambda h: h.dma_start(out=y_s, in_=xt[0:64, 0, :])], reads=[xt], store=True)
                else:
                    kb.dma(SP, [lambda h: h.dma_start(out=y_p[ti * 512:(ti + 1) * 512, :].rearrange("(j p) d -> p j d", p=128), in_=xt[:])],
                           reads=[xt], store=True)
            else:
                if samp:
                    kb.dma(SP, [lambda h: h.dma_start(out=x1[SEQ:SEQ + 64, :], in_=xt[0:64, 0, :])], reads=[xt], writes=[x1buf], sem_of=xt)
                else:
                    kb.dma(SP, [lambda h: h.dma_start(out=x1[ti * 512:(ti + 1) * 512, :].rearrange("(j p) d -> p j d", p=128), in_=xt[:])],
                           reads=[xt], writes=[x1buf], sem_of=xt)
    return


def _consts():
    ident = np.eye(128, dtype=np.float32)
    tri = (np.arange(128)[:, None] <= np.arange(128)[None, :]).astype(np.float32)
    half = 32
    inv = np.power(np.float32(10000.0), -np.arange(half, dtype=np.float32) / np.float32(half)).astype(np.float32)
    cs = np.zeros((2, 17, 128, 32), np.float32)
    pos = np.arange(SEQ, dtype=np.float32)
    ang = (pos[:, None] * inv[None, :]).astype(np.float32)
    cs[0, :16] = np.cos(ang).reshape(16, 128, 32)
    cs[1, :16] = np.sin(ang).reshape(16, 128, 32)
    pos_s = (SEQ + (np.arange(64) % 4)).astype(np.float32)
    ang_s = (pos_s[:, None] * inv[None, :]).astype(np.float32)
    cs[0, 16, :64] = np.cos(ang_s)
    cs[1, 16, :64] = np.sin(ang_s)
    pinv = np.zeros((4, 16), np.float32)
    for g, w in enumerate(WINS):
        pinv[g] = 1.0 / np.minimum(w, np.arange(16) + 1)
    key = np.arange(64)
    col = np.arange(128)
    cb, ct = col // 8, col % 4
    smask = ((key[:, None] // 4 == cb[None, :]) & (key[:, None] % 4 <= ct[None, :])).astype(np.float32)
    return ident, tri, cs, pinv, smask


_CACHE = {}


def kernel(**inputs):
    f = lambda a: np.ascontiguousarray(np.asarray(a))
    cache_k = f(inputs["cache_k"]); cache_v = f(inputs["cache_v"])
    n_pool = cache_k.shape[1]
    if n_pool not in _CACHE:
        _CACHE[n_pool] = build(n_pool)
    nc, _es = _CACHE[n_pool]
    ident, tri, cs, pinv, smask = _consts()
    ck = cache_k.reshape(DEPTH, n_pool * 128, BW)
    cv = cache_v.reshape(DEPTH, n_pool * 128, BW)
    ppack = np.concatenate([f(inputs["pool_scale"])[:, None, :], f(inputs["w_sconv"]), f(inputs["w_cconv"]),
                            f(inputs["b_cconv"])[:, None, :], f(inputs["g_cnorm"])[:, None, :],
                            f(inputs["b_cnorm"])[:, None, :]], axis=1).astype(np.float32)
    shared = {
        "ck0": ck[0], "ck1": ck[1], "cv0": cv[0], "cv1": cv[1], "w_ada": f(inputs["w_ada"]), "b_ada": f(inputs["b_ada"]),
        "g_pre": f(inputs["g_pre"]), "g_post": f(inputs["g_post"]), "w_in": f(inputs["w_in"]),
        "w_pool": f(inputs["w_pool"]), "ppack": np.ascontiguousarray(ppack),
        "lam_qk": f(inputs["lambda_qk"]).reshape(DEPTH, 256), "g_subln": f(inputs["g_subln"]),
        "w_branch": f(inputs["w_branch"]), "w_o": f(inputs["w_o"]),
        "c_ident": ident, "c_tri": tri, "c_cs": cs, "c_pinv": pinv, "c_smask": smask,
    }
    x_prompt = f(inputs["x_prompt"]); x_sample = f(inputs["x_sample"])
    c_prompt = f(inputs["c_prompt"]); c_sample = f(inputs["c_sample"])
    page_table = f(inputs["page_table"]).astype(np.int32)
    st_pool = f(inputs["state_pool"]); st_sconv = f(inputs["state_sconv"]); st_cconv = f(inputs["state_cconv"])
    in_maps = []
    for c in range(8):
        sb = slice(NSB * c, NSB * (c + 1))
        m = dict(shared)
        m["xp"] = x_prompt[c]
        m["xs"] = np.ascontiguousarray(x_sample[sb].reshape(NS, D))
        m["ptab"] = np.ascontiguousarray(page_table[sb])
        m["st_pool"] = np.ascontiguousarray(st_pool[:, sb]); m["st_sconv"] = np.ascontiguousarray(st_sconv[:, sb])
        m["st_cconv"] = np.ascontiguousarray(st_cconv[:, sb])
        m["c_all"] = np.ascontiguousarray(np.concatenate([c_prompt[c:c + 1], c_sample[sb]], axis=0))
        in_maps.append(m)
    res = run_bass_kernel_spmd(nc, in_maps, core_ids=list(range(8))).results
    g = lambda name: [np.asarray(r[name]) for r in res]
    y_prompt = np.stack(g("y_p"), 0)
    y_sample = np.concatenate(g("y_s"), 0).reshape(8 * NSB, 4, D)
    k_prompt = np.stack(g("k_p"), 1).reshape(DEPTH, 8, SEQ, 4, 2, 64)
    v_prompt = np.stack(g("v_p"), 1).reshape(DEPTH, 8, SEQ, 4, 128)
    pool_prompt = np.stack(g("pool_p"), 1)
    sconv_prompt = np.stack(g("sconv_p"), 1)
    cconv_prompt = np.stack(g("cconv_p"), 1)
    k_sample = np.concatenate([a.reshape(DEPTH, NSB, 4, 4, 2, 64) for a in g("k_s")], 1)
    v_sample = np.concatenate([a.reshape(DEPTH, NSB, 4, 4, 128) for a in g("v_s")], 1)
    pool_sample = np.concatenate(g("pool_s"), 1)
    sconv_sample = np.concatenate(g("sconv_s"), 1)
    cconv_sample = np.concatenate(g("cconv_s"), 1)
    outs = (y_prompt, y_sample, k_prompt, v_prompt, pool_prompt, sconv_prompt, cconv_prompt,
            k_sample, v_sample, pool_sample, sconv_sample, cconv_sample)
    return tuple(np.ascontiguousarray(o, dtype=np.float32) for o in outs)
```

```python
import numpy as np
from contextlib import ExitStack
import concourse.bass as bass
import concourse.mybir as mybir
from concourse.bass_utils import run_bass_kernel_spmd

F32 = mybir.dt.float32
BF16 = mybir.dt.bfloat16
I32 = mybir.dt.int32
ALU = mybir.AluOpType
AF = mybir.ActivationFunctionType
AX = mybir.AxisListType

D = 1024
BW = 512
SEQ = 2048
DEPTH = 2
NSB = 16
NS = 64
N_IN = 10752
EPS = 1e-6
WINS = (2, 4, 8, 16)
LAM_INIT = [0.8 - 0.6 * float(np.exp(-0.3 * l)) for l in range(DEPTH)]


class Buf:
    def __init__(self, name, psum=False):
        self.name = name
        self.psum = psum
        self.w = None
        self.r = []
        self.dsem = None
        self.dcnt = 0


class Eng:
    def __init__(self, h, sem, is_pe=False):
        self.h = h
        self.sem = sem
        self.cnt = 0
        self.seen = {}
        self.is_pe = is_pe


class KB:
    def __init__(self, nc, es):
        self.nc = nc
        self.es = es
        self.bufs = {}
        self._keep = []
        self.nsem = 0
        self.pe = Eng(nc.tensor, self.sem("pe"), True)
        self.act = Eng(nc.scalar, self.sem("act"))
        self.dve = Eng(nc.vector, self.sem("dve"))
        self.pool = Eng(nc.gpsimd, self.sem("pool"))
        self.sp = Eng(nc.sync, self.sem("sp"))
        self.stores = []

    def sem(self, name):
        self.nsem += 1
        return self.es.enter_context(self.nc.semaphore("s%d_%s" % (self.nsem, name)))

    def sb(self, name, shape, dt):
        t = self.es.enter_context(self.nc.sbuf_tensor(name, list(shape), dt))
        self.bufs[id(t)] = Buf(name)
        self._keep.append(t)
        return t

    def ps(self, name, shape, dt):
        t = self.es.enter_context(self.nc.psum_tensor(name, list(shape), dt))
        self.bufs[id(t)] = Buf(name, psum=True)
        self._keep.append(t)
        return t

    def buf(self, t):
        if isinstance(t, Buf):
            return t
        return self.bufs[id(t)]

    def _wait(self, eng, tk):
        if tk is None:
            return
        sem, val = tk
        key = id(sem)
        if eng.seen.get(key, 0) >= val:
            return
        if eng.is_pe and sem is eng.sem:
            return
        eng.h.wait_ge(sem, val)
        eng.seen[key] = val

    def _deps(self, eng, reads, writes):
        for t in reads:
            self._wait(eng, self.buf(t).w)
        for t in writes:
            b = self.buf(t)
            self._wait(eng, b.w)
            for tk in b.r:
                self._wait(eng, tk)

    def op(self, eng, fn, reads=(), writes=(), inc=True):
        pr = [t for t in reads if self.buf(t).psum]
        if pr:
            reads = [t for t in reads if not self.buf(t).psum]
            writes = list(writes) + [t for t in pr if t not in writes]
        self._deps(eng, reads, writes)
        ins = fn(eng.h)
        if inc:
            ins.then_inc(eng.sem, 1)
            eng.cnt += 1
            tk = (eng.sem, eng.cnt)
        else:
            tk = (eng.sem, eng.cnt + 1)
        for t in reads:
            self.buf(t).r.append(tk)
        for t in writes:
            b = self.buf(t)
            b.w = tk
            b.r = []
        return tk

    def dma(self, eng, fns, reads=(), writes=(), sem_of=None, store=False):
        self._deps(eng, reads, writes)
        b = self.buf(sem_of if sem_of is not None else (writes[0] if writes else reads[0]))
        if b.dsem is None:
            b.dsem = self.sem("d_" + b.name)
        for fn in fns:
            fn(eng.h).then_inc(b.dsem, 16)
            b.dcnt += 16
        tk = (b.dsem, b.dcnt)
        for t in reads:
            self.buf(t).r.append(tk)
        for t in writes:
            bb = self.buf(t)
            bb.w = tk
            bb.r = []
        if store:
            self.stores.append(tk)
        return tk

    def view(self, name, ap):
        self.bufs[id(ap)] = Buf(name)
        self._keep.append(ap)
        return ap

    def barrier(self):
        engs = (self.pe, self.act, self.dve, self.pool, self.sp)
        dts = [(b.dsem, b.dcnt) for b in self.bufs.values() if b.dsem is not None and b.dcnt]
        dts += [(b.dsem, b.dcnt) for b in getattr(self, "xbufs", []) if b.dsem is not None and b.dcnt]
        for e in engs:
            for f in engs:
                if f is not e and f.cnt:
                    self._wait(e, (f.sem, f.cnt))
            for tk in dts:
                self._wait(e, tk)

    def finish(self):
        for tk in self.stores:
            self._wait(self.sp, tk)
        for e in (self.pe, self.act, self.dve, self.pool):
            if e.cnt:
                self._wait(self.sp, (e.sem, e.cnt))


class _Stop(Exception):
    pass


_STOP = [0]


def build(n_pool):
    st = {}
    try:
        _build_inner(n_pool, st)
    except _Stop:
        pass
    st['kb'].finish()
    return st['nc'], st['es']


def _build_inner(n_pool, st):
    nc = bass.Bass("TRN2", target_bir_lowering=False)
    es = ExitStack()
    kb = KB(nc, es)
    st.update(nc=nc, es=es, kb=kb)
    PE, ACT, DVE, POOL, SP = kb.pe, kb.act, kb.dve, kb.pool, kb.sp

    def din(name, shape, dt=F32):
        return nc.dram_tensor(name, list(shape), dt, kind="ExternalInput").ap()

    def dout(name, shape, dt=F32):
        return nc.dram_tensor(name, list(shape), dt, kind="ExternalOutput").ap()

    xp = din("xp", [SEQ, D]); xs = din("xs", [NS, D])
    ck = [din(f"ck{i}", [n_pool * 128, BW]) for i in range(DEPTH)]; cv = [din(f"cv{i}", [n_pool * 128, BW]) for i in range(DEPTH)]
    breg = es.enter_context(nc.gpsimd.register("bc"))
    nc.gpsimd.reg_mov(breg, n_pool * 128 - 1)
    ptab = din("ptab", [NSB, 16], I32)
    st_pool = din("st_pool", [DEPTH, NSB, 15, BW]); st_sconv = din("st_sconv", [DEPTH, NSB, 2, BW])
    st_cconv = din("st_cconv", [DEPTH, NSB, 30, BW])
    c_all = din("c_all", [17, D])
    w_ada = din("w_ada", [DEPTH, D, 3 * D]); b_ada = din("b_ada", [DEPTH, 3 * D])
    g_pre = din("g_pre", [DEPTH, D]); g_post = din("g_post", [DEPTH, D])
    w_in = din("w_in", [DEPTH, D, N_IN]); w_pool = din("w_pool", [DEPTH, 4, 128, 128])
    ppack = din("ppack", [DEPTH, 38, BW]); lam_qk = din("lam_qk", [DEPTH, 256])
    g_subln = din("g_subln", [DEPTH, 128])
    w_branch = din("w_branch", [DEPTH, 4, BW, D]); w_o = din("w_o", [DEPTH, D, D])
    c_ident = din("c_ident", [128, 128]); c_tri = din("c_tri", [128, 128])
    c_cs = din("c_cs", [2, 17, 128, 32]); c_pinv = din("c_pinv", [4, 16])
    c_smask = din("c_smask", [64, 128])

    y_p = dout("y_p", [SEQ, D]); y_s = dout("y_s", [NS, D])
    k_p = dout("k_p", [DEPTH, SEQ, BW]); v_p = dout("v_p", [DEPTH, SEQ, BW])
    pool_p = dout("pool_p", [DEPTH, 15, BW]); sconv_p = dout("sconv_p", [DEPTH, 2, BW])
    cconv_p = dout("cconv_p", [DEPTH, 30, BW])
    k_s = dout("k_s", [DEPTH, NS, BW]); v_s = dout("v_s", [DEPTH, NS, BW])
    pool_s = dout("pool_s", [DEPTH, NSB, 15, BW]); sconv_s = dout("sconv_s", [DEPTH, NSB, 2, BW])
    cconv_s = dout("cconv_s", [DEPTH, NSB, 30, BW])
    x1 = nc.dram_tensor("x1_scratch", [SEQ + NS, D], F32, kind="Internal").ap()
    x1buf = Buf("x1dram")

    xt = kb.sb("xt", [128, 4, D], F32)
    kT = kb.sb("kT", [128, 4, SEQ + NS], BF16)
    vtok = kb.sb("vtok", [128, 17, BW], BF16)
    NW = 3
    wb = [kb.sb(f"wb{i}", [128, 8, 512], BF16) for i in range(NW)]
    wctr = [0]
    ident = kb.sb("ident", [128, 128], F32); identb = kb.sb("identb", [128, 128], BF16)
    trib = kb.sb("trib", [128, 128], BF16); onesb = kb.sb("onesb", [128, 128], BF16)
    trif = kb.sb("trif", [128, 128], F32)
    smask = kb.sb("smask", [64, 128], BF16)
    cs = kb.sb("cs", [128, 2, 17, 32], F32)
    pinv = kb.sb("pinv", [128, 4, 16], F32)
    epsb = kb.sb("epsb", [128, 1], F32)
    scT = kb.sb("scT", [128, 8, 17], BF16)
    sc_p = kb.sb("sc_p", [128, 8, 128], BF16); sc_s = kb.sb("sc_s", [128, 8, 64], BF16)
    modT = kb.sb("modT", [128, 24, 17], F32)
    opsT = kb.sb("opsT", [128, 8, 17], F32)
    gate_p = kb.sb("gate_p", [128, D], F32); gate_s = kb.sb("gate_s", [64, D], F32)
    gpre_b = kb.sb("gpre_b", [128, D], F32); gpost_b = kb.sb("gpost_b", [128, D], F32)
    badab = kb.sb("badab", [1, 3 * D], BF16)
    pp = kb.sb("pp", [128, 4, 38], F32)
    gsub = kb.sb("gsub", [128, 1], F32)
    lq = kb.sb("lq", [128, 256], F32); lamt = kb.sb("lamt", [128, 4], F32)
    wpool = kb.sb("wpool", [128, 4, 128], BF16)
    idx = kb.sb("idx", [128, 256], I32)
    halo_u = kb.sb("halo_u", [128, 4, 15], F32); halo_p = kb.sb("halo_p", [128, 4, 2], F32)
    halo_g = kb.sb("halo_g", [128, 4, 30], F32)
    tok1 = kb.sb("tok1", [128, D], F32); tok2 = kb.sb("tok2", [128, D], BF16)
    tok3 = kb.sb("tok3", [128, 1536], F32)
    small = kb.sb("small", [128, 8], F32)
    stnew = kb.sb("stnew", [128, 3, 4, 64], F32)
    sttok = kb.sb("sttok", [128, BW], F32)
    AW = 18752
    arena = kb.sb("arena", [128, AW], F32)
    aptr = [0]

    def carve(name, shape, dt):
        n = int(np.prod(shape[1:]))
        nf = n if dt == F32 else (n + 1) // 2
        ap = arena[:, aptr[0]:aptr[0] + nf]
        aptr[0] += nf
        assert aptr[0] <= AW, (name, aptr[0])
        if dt != F32:
            ap = ap.bitcast(dt)
        ap = ap[:, 0:n]
        if len(shape) == 3:
            ap = ap.rearrange("p (a b) -> p a b", a=shape[1])
        elif len(shape) == 4:
            ap = ap.rearrange("p (a b c) -> p a b c", a=shape[1], b=shape[2])
        elif len(shape) == 5:
            ap = ap.rearrange("p (a b c d) -> p a b c d", a=shape[1], b=shape[2], c=shape[3])
        return kb.view(name, ap)

    pg = [kb.ps(f"pg{i}", [128, 512], F32) for i in range(2)]
    psc = [kb.ps(f"psc{i}", [128, 512], F32) for i in range(2)]
    pacc = [kb.ps(f"pacc{i}", [128, 512], F32) for i in range(4)]
    pgc = [0]; pscc = [0]
    in_attn = [False]
    pg4 = [pg[0], pg[1], psc[0], psc[1]]

    def next_pg():
        pgc[0] += 1
        if in_attn[0]:
            return pg[pgc[0] % 2]
        return pg4[pgc[0] % 4]

    def next_psc():
        pscc[0] += 1
        return psc[pscc[0] % 2]

    def load(eng, dst, dst_ap, src_ap):
        return kb.dma(eng, [lambda h: h.dma_start(out=dst_ap, in_=src_ap)], writes=[dst])

    def wload(src3):
        w = wb[wctr[0] % NW]
        wctr[0] += 1
        kc, cols = src3.shape[1], src3.shape[2]
        kb.dma(POOL, [lambda h: h.dma_start(out=w[:, 0:kc, 0:cols], in_=src3)], writes=[w])
        return w

    def mm_group(out_ap, pairs, reads, wr):
        n = len(pairs)
        for i, (lt, rh) in enumerate(pairs):
            kb.op(PE, lambda h, lt=lt, rh=rh, i=i: h.matmul(out_ap, lt, rh, start=(i == 0), stop=(i == n - 1)),
                  reads=reads, writes=[wr], inc=(i == n - 1))

    def transpose_to(dst_ps_ap, src_ap, idt, reads, wr, inc=True):
        kb.op(PE, lambda h: h.transpose(dst_ps_ap, src_ap, idt), reads=reads, writes=[wr], inc=inc)

    def v3(ap2, nb):
        return ap2.rearrange("p (b t) -> p b t", b=nb)

    def cut(n, cond=True):
        if cond and _STOP[0] == n:
            raise _Stop()

    load(SP, ident, ident[:], c_ident)
    load(SP, trif, trif[:], c_tri)
    load(SP, cs, cs[:], c_cs.rearrange("a b p c -> p a b c"))
    load(SP, pinv, pinv[:].rearrange("p g t -> p (g t)"), c_pinv.rearrange("g t -> (g t)").partition_broadcast(128))
    kb.dma(POOL, [lambda h: h.dma_start(out=smask[:], in_=c_smask)], writes=[smask])
    kb.op(DVE, lambda h: h.tensor_copy(out=identb[:], in_=ident[:]), reads=[ident], writes=[identb])
    kb.op(DVE, lambda h: h.tensor_copy(out=trib[:], in_=trif[:]), reads=[trif], writes=[trib])
    kb.op(DVE, lambda h: h.memset(onesb[:], 1.0), writes=[onesb])
    kb.op(DVE, lambda h: h.memset(epsb[:], EPS), writes=[epsb])
    pti = tok3[:, 0:256].bitcast(I32)
    ptf = tok1[:, 0:256]
    offi = tok3[:, 256:257].bitcast(I32)
    offf = tok1[:, 256:257]
    load(SP, tok3, pti, ptab.rearrange("b j -> (b j)").partition_broadcast(128))
    kb.op(POOL, lambda h: h.iota(offi, pattern=[[0, 1]], base=0, channel_multiplier=1), reads=[], writes=[tok3])
    kb.op(DVE, lambda h: h.tensor_copy(out=offf, in_=offi), reads=[tok3], writes=[tok1])
    kb.op(DVE, lambda h: h.tensor_copy(out=ptf, in_=pti), reads=[tok3], writes=[tok1])
    kb.op(DVE, lambda h: h.tensor_scalar(out=ptf, in0=ptf, scalar1=128.0, scalar2=offf,
                                         op0=ALU.mult, op1=ALU.add), reads=[tok1], writes=[tok1])
    kb.op(DVE, lambda h: h.tensor_copy(out=idx[:], in_=ptf), reads=[tok1], writes=[idx])
    load(SP, tok1, tok1[0:17, :], c_all)
    kb.op(ACT, lambda h: h.activation(out=tok2[0:17, :], in_=tok1[0:17, :], func=AF.Silu), reads=[tok1], writes=[tok2])
    p = next_pg()
    pbf = p[:].bitcast(BF16)
    for kc in range(8):
        transpose_to(pbf[:, kc * 32:kc * 32 + 17], tok2[0:17, kc * 128:(kc + 1) * 128], identb[0:17, 0:17],
                     [tok2, identb], p, inc=(kc == 7))
    kb.op(DVE, lambda h: h.tensor_copy(out=scT[:], in_=pbf[:, 0:256].rearrange("p (k c) -> p k c", c=32)[:, :, 0:17]),
          reads=[p], writes=[scT])
    kb.op(DVE, lambda h: h.tensor_copy(out=sc_p[:], in_=scT[:, :, 0:1].to_broadcast([128, 8, 128])),
          reads=[scT], writes=[sc_p])
    kb.op(DVE, lambda h: h.tensor_copy(out=sc_s[:].rearrange("p k (b t) -> p k b t", t=4),
                                       in_=scT[:, :, 1:17].unsqueeze(3).to_broadcast([128, 8, 16, 4])),
          reads=[scT], writes=[sc_s])

    cut(1)
    for l in range(DEPTH):
        lam_init = LAM_INIT[l]
        load(SP, gpre_b, gpre_b[:], g_pre[l].partition_broadcast(128))
        load(SP, gpost_b, gpost_b[:], g_post[l].partition_broadcast(128))
        kb.dma(POOL, [lambda h: h.dma_start(out=badab[:], in_=b_ada[l:l + 1, :])], writes=[badab])
        load(SP, gsub, gsub[:], g_subln[l].rearrange("(p o) -> p o", o=1))
        load(SP, lq, lq[:], lam_qk[l].partition_broadcast(128))
        load(SP, tok1, tok1[0:38, 0:BW], ppack[l])
        p = next_pg()
        for c in range(4):
            transpose_to(p[:, c * 38:(c + 1) * 38], tok1[0:38, c * 128:(c + 1) * 128], ident[0:38, 0:38],
                         [tok1, ident], p, inc=(c == 3))
        kb.op(DVE, lambda h: h.tensor_copy(out=pp[:], in_=p[:, 0:152].rearrange("p (c r) -> p c r", r=38)),
              reads=[p], writes=[pp])
        kb.dma(POOL, [lambda h: h.dma_start(out=wpool[:], in_=w_pool[l].rearrange("g c e -> c g e"))], writes=[wpool])
        lq4 = lq[:].rearrange("p (a b d) -> p a b d", a=2, b=2)
        kb.op(DVE, lambda h: h.tensor_tensor(out=tok1[:, 0:128].rearrange("p (a d) -> p a d", a=2),
                                             in0=lq4[:, :, 0, :], in1=lq4[:, :, 1, :], op=ALU.mult), reads=[lq], writes=[tok1])
        kb.op(DVE, lambda h: h.tensor_reduce(out=lamt[:, 0:2], in_=tok1[:, 0:128].rearrange("p (a d) -> p a d", a=2),
                                             axis=AX.X, op=ALU.add), reads=[tok1], writes=[lamt])
        kb.op(ACT, lambda h: h.activation(out=lamt[:, 0:2], in_=lamt[:, 0:2], func=AF.Exp), reads=[lamt], writes=[lamt])
        kb.op(DVE, lambda h: h.tensor_tensor(out=lamt[:, 2:3], in0=lamt[:, 1:2], in1=lamt[:, 0:1], op=ALU.subtract),
              reads=[lamt], writes=[lamt])
        kb.op(DVE, lambda h: h.tensor_scalar(out=lamt[:, 2:3], in0=lamt[:, 2:3], scalar1=-lam_init, scalar2=None,
                                             op0=ALU.add), reads=[lamt], writes=[lamt])
        for ci in range(6):
            w = wload(w_ada[l][:, ci * 512:(ci + 1) * 512].rearrange("(kc p) c -> p kc c", p=128))
            p = next_pg()
            for j in range(4):
                m = ci * 4 + j
                pairs = [(w[:, kc, j * 128:(j + 1) * 128], scT[:, kc, :]) for kc in range(8)]
                pairs.append((badab[0:1, m * 128:(m + 1) * 128], onesb[0:1, 0:17]))
                mm_group(p[:, j * 32:j * 32 + 17], pairs, [w, scT, badab, onesb], p)
            kb.op(DVE, lambda h: h.tensor_copy(
                out=modT[:, ci * 4:(ci + 1) * 4, :], in_=p[:, 0:128].rearrange("p (j c) -> p j c", c=32)[:, :, 0:17]),
                reads=[p], writes=[modT])
            if ci >= 4:
                half = ci - 4
                p = next_pg()
                pairs = [(sc_p[:, kc, :], w[:, kc, :]) for kc in range(8)]
                pairs.append((onesb[0:1, 0:128], badab[0:1, ci * 512:(ci + 1) * 512]))
                mm_group(p[:, :], pairs, [w, sc_p, badab, onesb], p)
                kb.op(ACT, lambda h: h.copy(out=gate_p[:, half * 512:(half + 1) * 512], in_=p[:, :]),
                      reads=[p], writes=[gate_p])
                p = next_pg()
                pairs = [(sc_s[:, kc, :], w[:, kc, :]) for kc in range(8)]
                pairs.append((onesb[0:1, 0:64], badab[0:1, ci * 512:(ci + 1) * 512]))
                mm_group(p[0:64, :], pairs, [w, sc_s, badab, onesb], p)
                kb.op(ACT, lambda h: h.copy(out=gate_s[:, half * 512:(half + 1) * 512], in_=p[0:64, :]),
                      reads=[p], writes=[gate_s])
        kb.op(DVE, lambda h: h.tensor_scalar(out=opsT[:], in0=modT[:, 8:16, :], scalar1=1.0, scalar2=None, op0=ALU.add),
              reads=[modT], writes=[opsT])

        cut(2, l == 0)

        def win_chunk(ci):
            return w_in[l][:, ci * 512:(ci + 1) * 512].rearrange("(kc p) c -> p kc c", p=128)

        for ti in range(5):
            samp = (ti == 4)
            NT = 64 if samp else 512
            nblk = 1 if samp else 4
            rows = 64 if samp else 128
            nbv = NSB if samp else 1
            if ti == 0 or samp:
                kb.barrier()
                aptr[0] = 0
                hT = carve("hT", [128, 8, NT], BF16)
                merged = carve("merged", [128, 8, NT], F32)
                ys = carve("ys", [128, 4, NT], BF16)
                qT = carve("qT", [128, 4, NT], BF16)
                sg = carve("sg", [128, 4, NT], BF16)
                cc = carve("cc", [128, 4, NT], F32)
                cgs = carve("cgs", [128, 4, NT], F32)
                bgs = carve("bgs", [128, 4, NT], BF16)
                tA = carve("tA", [128, 576], F32); tB = carve("tB", [128, 576], F32)
                tC = carve("tC", [128, NT], F32); tD = carve("tD", [128, NT], F32); tE = carve("tE", [128, NT], F32)
                tb1 = carve("tb1", [128, 512], BF16); tb2 = carve("tb2", [128, 512], BF16)
                tC2 = carve("tC2", [128, NT], F32); tD2 = carve("tD2", [128, NT], F32)
                if samp:
                    stfm_u = carve("stfm_u", [128, 4, NSB * 15], F32)
                    stfm_p = carve("stfm_p", [128, 4, NSB * 2], F32)
                    stfm_g = carve("stfm_g", [128, 4, NSB * 30], F32)
                    kpg = [carve(f"kpg{i}", [128, 4, BW], BF16) for i in range(2)]
                    vpg = [carve(f"vpg{i}", [128, 4, BW], BF16) for i in range(2)]
                    kTs = carve("kTs", [128, 4, 4, 128], BF16)
                    Qall = carve("Qall", [128, 4, NSB, 2, 4], BF16)
                    pTs = carve("pTs", [128, 16, 32], BF16)
                    pTn = carve("pTn", [128, 32], BF16)
                    oS = carve("oS", [128, 4, 64], F32)
                    rinv = carve("rinv", [128, 32], F32)
                else:
                    stfm_u = stfm_p = stfm_g = None

            def ipf(w, j):
                pq = next_pg()
                mm_group(pq[:, 0:NT], [(w[:, kc, j * 128:(j + 1) * 128], hT[:, kc, 0:NT]) for kc in range(8)], [w, hT], pq)
                return pq

            if l == 0:
                if samp:
                    load(SP, xt, xt[0:64, 0, :], xs)
                else:
                    load(SP, xt, xt[:], xp[ti * 512:(ti + 1) * 512, :].rearrange("(j p) d -> p j d", p=128))
            else:
                if samp:
                    kb.dma(SP, [lambda h: h.dma_start(out=xt[0:64, 0, :], in_=x1[SEQ:SEQ + 64, :])], reads=[x1buf], writes=[xt])
                else:
                    kb.dma(SP, [lambda h: h.dma_start(out=xt[:], in_=x1[ti * 512:(ti + 1) * 512, :].rearrange("(j p) d -> p j d", p=128))],
                           reads=[x1buf], writes=[xt])

            for j in range(nblk):
                kb.op(DVE, lambda h: h.memset(small[:, 0:1], 0.0), writes=[small])
                kb.op(ACT, lambda h: h.activation(out=tok2[0:rows, :], in_=xt[0:rows, j, :], func=AF.Square,
                                                  accum_out=small[0:rows, 0:1]), reads=[xt, small], writes=[tok2, small])
                kb.op(ACT, lambda h: h.activation(out=small[0:rows, 1:2], in_=small[0:rows, 0:1], func=AF.Sqrt,
                                                  bias=epsb[0:rows, 0:1], scale=1.0 / D), reads=[small, epsb], writes=[small])
                kb.op(DVE, lambda h: h.reciprocal(out=small[0:rows, 2:3], in_=small[0:rows, 1:2]), reads=[small], writes=[small])
                kb.op(DVE, lambda h: h.scalar_tensor_tensor(out=tok2[0:rows, :], in0=xt[0:rows, j, :], scalar=small[0:rows, 2:3],
                                                            in1=gpre_b[0:rows, :], op0=ALU.mult, op1=ALU.mult),
                      reads=[xt, small, gpre_b], writes=[tok2])
                p = next_pg()
                pbf = p[:].bitcast(BF16)
                for kc in range(8):
                    transpose_to(pbf[:, kc * 128:kc * 128 + rows], tok2[0:rows, kc * 128:(kc + 1) * 128],
                                 identb[0:rows, 0:rows], [tok2, identb], p, inc=(kc == 7))
                src = pbf[:, :].rearrange("p (k t) -> p k t", t=128)[:, :, 0:rows]
                if not samp:
                    t38 = tok3[:, 0:1024].rearrange("p (k t) -> p k t", t=128)
                    kb.op(DVE, lambda h: h.tensor_tensor(out=t38, in0=src, in1=opsT[:, :, 0:1].to_broadcast([128, 8, 128]), op=ALU.mult),
                          reads=[p, opsT], writes=[tok3])
                    kb.op(DVE, lambda h: h.tensor_tensor(out=hT[:, :, j * 128:(j + 1) * 128], in0=t38,
                                                         in1=modT[:, 0:8, 0:1].to_broadcast([128, 8, 128]), op=ALU.add),
                          reads=[tok3, modT], writes=[hT])
                else:
                    t38 = tok3[:, 0:512].rearrange("p (k b t) -> p k b t", k=8, t=4)
                    kb.op(DVE, lambda h: h.tensor_tensor(out=t38, in0=src.rearrange("p k (b t) -> p k b t", t=4),
                                                         in1=opsT[:, :, 1:17].unsqueeze(3).to_broadcast([128, 8, 16, 4]), op=ALU.mult),
                          reads=[p, opsT], writes=[tok3])
                    kb.op(DVE, lambda h: h.tensor_tensor(out=hT[:, :, 0:64].rearrange("p k (b t) -> p k b t", t=4), in0=t38,
                                                         in1=modT[:, 0:8, 1:17].unsqueeze(3).to_broadcast([128, 8, 16, 4]), op=ALU.add),
                          reads=[tok3, modT], writes=[hT])

            cut(31, l == 0 and ti == 0)
            wq = wload(win_chunk(6)); wk = wload(win_chunk(7))
            for j in range(nblk):
                xb = ti * 4 + j
                for which, w in ((0, wq), (1, wk)):
                    p = next_pg()
                    mm_group(p[0:rows, :], [(hT[:, kc, j * 128:j * 128 + rows], w[:, kc, :]) for kc in range(8)], [hT, w], p)
                    cosb = cs[0:rows, 0, xb, :].unsqueeze(1).to_broadcast([rows, 8, 32])
                    sinb = cs[0:rows, 1, xb, :].unsqueeze(1).to_broadcast([rows, 8, 32])
                    o3 = tok3[0:rows, 0:512].rearrange("p (g two d) -> p g two d", two=2, d=32)
                    t3 = tok3[0:rows, 512:1024].rearrange("p (g two d) -> p g two d", two=2, d=32)
                    kb.op(ACT, lambda h: h.copy(out=tok1[0:rows, 0:512], in_=p[0:rows, :]), reads=[p], writes=[tok1])
                    s3 = tok1[0:rows, 0:512].rearrange("p (g two d) -> p g two d", two=2, d=32)
                    cut(33, l == 0 and ti == 0)
                    for half in range(2):
                        kb.op(DVE, lambda h: h.tensor_tensor(out=o3[:, :, half, :], in0=s3[:, :, half, :], in1=cosb, op=ALU.mult),
                              reads=[tok1, cs], writes=[tok3])
                        kb.op(DVE, lambda h: h.tensor_tensor(out=t3[:, :, half, :], in0=s3[:, :, 1 - half, :], in1=sinb, op=ALU.mult),
                              reads=[tok1, cs], writes=[tok3])
                    kb.op(DVE, lambda h: h.tensor_tensor(out=o3[:, :, 0, :], in0=o3[:, :, 0, :], in1=t3[:, :, 0, :], op=ALU.subtract),
                          reads=[tok3], writes=[tok3])
                    kb.op(DVE, lambda h: h.tensor_tensor(out=o3[:, :, 1, :], in0=o3[:, :, 1, :], in1=t3[:, :, 1, :], op=ALU.add),
                          reads=[tok3], writes=[tok3])
                    cut(34, l == 0 and ti == 0)
                    kb.op(ACT, lambda h: h.copy(out=tok2[0:rows, 0:512], in_=tok3[0:rows, 0:512]), reads=[tok3], writes=[tok2])
                    if which == 1:
                        dst = k_s[l] if samp else k_p[l][xb * 128:(xb + 1) * 128, :]
                        kb.dma(SP, [lambda h: h.dma_start(out=dst, in_=tok3[0:rows, 0:512])], reads=[tok3], store=True)
                    pt = next_pg()
                    ptb = pt[:].bitcast(BF16)
                    for hh in range(4):
                        transpose_to(ptb[:, hh * 128:hh * 128 + rows], tok2[0:rows, hh * 128:(hh + 1) * 128],
                                     identb[0:rows, 0:rows], [tok2, identb], pt, inc=(hh == 3))
                    srcT = ptb[:, 0:512].rearrange("p (h t) -> p h t", t=128)[:, :, 0:rows]
                    if which == 0:
                        kb.op(ACT, lambda h: h.copy(out=qT[:, :, j * 128:j * 128 + rows], in_=srcT), reads=[pt], writes=[qT])
                    else:
                        kb.op(ACT, lambda h: h.copy(out=kT[:, :, xb * 128:xb * 128 + rows], in_=srcT), reads=[pt], writes=[kT])
                    cut(35, l == 0 and ti == 0 and which == 0)
                    cut(36, l == 0 and ti == 0 and which == 1)
            cut(32, l == 0 and ti == 0)
            wv = wload(win_chunk(8))
            for j in range(nblk):
                xb = ti * 4 + j
                p = next_pg()
                mm_group(p[0:rows, :], [(hT[:, kc, j * 128:j * 128 + rows], wv[:, kc, :]) for kc in range(8)], [hT, wv], p)
                kb.op(ACT, lambda h: h.copy(out=tok3[0:rows, 1024:1536], in_=p[0:rows, :]), reads=[p], writes=[tok3])
                kb.op(DVE, lambda h: h.tensor_copy(out=vtok[0:rows, xb, :], in_=p[0:rows, :]), reads=[p], writes=[vtok])
                dst = v_s[l] if samp else v_p[l][xb * 128:(xb + 1) * 128, :]
                kb.dma(SP, [lambda h: h.dma_start(out=dst, in_=tok3[0:rows, 1024:1536])], reads=[tok3], store=True)

            cut(3, l == 0 and ti == 0)
            cut(13, l == 0 and ti == 4)

            def emit_rows(get_ap, n, dsts):
                pe_ = next_pg()
                for c in range(4):
                    transpose_to(pe_[0:n, c * 128:(c + 1) * 128], get_ap(c), ident[:, :], [stnew, ident], pe_, inc=(c == 3))
                kb.op(ACT, lambda h: h.copy(out=sttok[0:n, :], in_=pe_[0:n, :]), reads=[pe_], writes=[sttok])
                kb.dma(SP, [(lambda h, r0=r0, nr=nr, d_=d_: h.dma_start(out=d_, in_=sttok[r0:r0 + nr, :])) for (r0, nr, d_) in dsts],
                       reads=[sttok], store=True)

            def branch_merge(n):
                for half in range(2):
                    wm = wload(win_chunk(13 + 2 * n + half))
                    wbr = wload(w_branch[l, n][:, half * 512:(half + 1) * 512].rearrange("(kc p) c -> p kc c", p=128))
                    for jj in range(4):
                        dc = half * 4 + jj
                        tCx, tDx = (tC, tD) if dc % 2 == 0 else (tC2, tD2)
                        pm = ipf(wm, jj)
                        kb.op(ACT, lambda h: h.activation(out=tCx[:, 0:NT], in_=pm[:, 0:NT], func=AF.Sigmoid), reads=[pm], writes=[tCx])
                        p2 = next_pg()
                        mm_group(p2[:, 0:NT], [(wbr[:, kc, jj * 128:(jj + 1) * 128], ys[:, kc, 0:NT]) for kc in range(4)], [wbr, ys], p2)
                        if n == 0:
                            kb.op(DVE, lambda h: h.tensor_tensor(out=merged[:, dc, 0:NT], in0=p2[:, 0:NT], in1=tCx[:, 0:NT], op=ALU.mult),
                                  reads=[p2, tCx], writes=[merged])
                        else:
                            kb.op(DVE, lambda h: h.tensor_tensor(out=tDx[:, 0:NT], in0=p2[:, 0:NT], in1=tCx[:, 0:NT], op=ALU.mult),
                                  reads=[p2, tCx], writes=[tDx])
                            kb.op(DVE, lambda h: h.tensor_tensor(out=merged[:, dc, 0:NT], in0=merged[:, dc, 0:NT], in1=tDx[:, 0:NT], op=ALU.add),
                                  reads=[merged, tDx], writes=[merged])

            def silu_gates(n):
                wg = wload(win_chunk(9 + n))
                for c in range(4):
                    pq = ipf(wg, c)
                    kb.op(ACT, lambda h: h.activation(out=sg[:, c, 0:NT], in_=pq[:, 0:NT], func=AF.Silu), reads=[pq], writes=[sg])

            def load_state_fm(src2d, nrows, dst):
                r0 = 0
                while r0 < nrows:
                    nr = min(128, nrows - r0)
                    load(SP, tok1, tok1[0:nr, 0:BW], src2d[r0:r0 + nr, :])
                    pq = next_pg()
                    for c in range(4):
                        transpose_to(pq[:, c * 128:c * 128 + nr], tok1[0:nr, c * 128:(c + 1) * 128], ident[0:nr, 0:nr],
                                     [tok1, ident], pq, inc=(c == 3))
                    kb.op(ACT, lambda h: h.copy(out=dst[:, :, r0:r0 + nr],
                                                in_=pq[:, :].rearrange("p (c t) -> p c t", t=128)[:, :, 0:nr]),
                          reads=[pq], writes=[dst])
                    r0 += nr

            def fill_seq(buf, g, npre, halo, stfm, src_fn):
                if samp:
                    seq = buf[:, 0:NSB * (npre + 4)].rearrange("p (b t) -> p b t", t=npre + 4)
                    L = 4
                    kb.op(DVE, lambda h: h.tensor_copy(out=seq[:, :, 0:npre],
                                                       in_=stfm[:, g, :].rearrange("p (b r) -> p b r", r=npre)),
                          reads=[stfm], writes=[buf])
                else:
                    seq = buf[:, 0:npre + NT].rearrange("p (b t) -> p b t", b=1)
                    L = NT
                    if ti == 0:
                        kb.op(DVE, lambda h: h.memset(seq[:, :, 0:npre], 0.0), writes=[buf])
                    else:
                        kb.op(DVE, lambda h: h.tensor_copy(out=seq[:, 0, 0:npre], in_=halo[:, g, :]), reads=[halo], writes=[buf])
                src_fn(seq[:, :, npre:npre + L])
                if not samp:
                    kb.op(DVE, lambda h: h.tensor_copy(out=halo[:, g, :], in_=seq[:, 0, L:L + npre]), reads=[buf], writes=[halo])
                return seq, L

            if samp:
                load_state_fm(st_pool[l].rearrange("b r c -> (b r) c"), NSB * 15, stfm_u)
                load_state_fm(st_sconv[l].rearrange("b r c -> (b r) c"), NSB * 2, stfm_p)
                load_state_fm(st_cconv[l].rearrange("b r c -> (b r) c"), NSB * 30, stfm_g)
                dd = Buf(f"stcopy{l}")
                kb.dma(SP, [lambda h: h.dma_start(out=pool_s[l][:, 0:11, :], in_=st_pool[l][:, 4:15, :]),
                            lambda h: h.dma_start(out=cconv_s[l][:, 0:26, :], in_=st_cconv[l][:, 4:30, :])],
                       sem_of=dd, store=True)
            want_state = samp or ti == 3

            silu_gates(0)
            wu = wload(win_chunk(0))
            for g in range(4):
                win = WINS[g]
                p = ipf(wu, g)
                seq, L = fill_seq(tA, g, 15, halo_u, stfm_u,
                                  lambda o: kb.op(ACT, lambda h: h.copy(out=o, in_=v3(p[:, 0:NT], nbv)), reads=[p], writes=[tA]))
                if want_state:
                    ncol = 64 if samp else 15
                    c0 = 0 if samp else NT - 15
                    kb.op(ACT, lambda h: h.copy(out=stnew[:, 0, g, 0:ncol], in_=p[:, c0:c0 + ncol]), reads=[p], writes=[stnew])
                acc = v3(tB[:, 0:NT], nbv)
                kb.op(DVE, lambda h: h.tensor_tensor(out=acc, in0=seq[:, :, 15:15 + L], in1=seq[:, :, 14:14 + L], op=ALU.add),
                      reads=[tA], writes=[tB])
                for i in range(2, win):
                    kb.op(DVE, lambda h: h.tensor_tensor(out=acc, in0=acc, in1=seq[:, :, 15 - i:15 - i + L], op=ALU.add),
                          reads=[tA, tB], writes=[tB])
                kb.op(DVE, lambda h: h.scalar_tensor_tensor(out=v3(tb1[:, 0:NT], nbv), in0=acc, scalar=1.0 / win,
                                                            in1=seq[:, :, 15:15 + L], op0=ALU.mult, op1=ALU.subtract),
                      reads=[tA, tB], writes=[tb1])
                if ti == 0:
                    kb.op(DVE, lambda h: h.tensor_tensor(out=tC[:, 0:16], in0=tB[:, 0:16], in1=pinv[:, g, :], op=ALU.mult),
                          reads=[tB, pinv], writes=[tC])
                    kb.op(DVE, lambda h: h.tensor_tensor(out=tb1[:, 0:16], in0=tC[:, 0:16], in1=tA[:, 15:31], op=ALU.subtract),
                          reads=[tC, tA], writes=[tb1])
                p2 = next_pg()
                mm_group(p2[:, 0:NT], [(wpool[:, g, :], tb1[:, 0:NT])], [wpool, tb1], p2)
                kb.op(DVE, lambda h: h.scalar_tensor_tensor(out=ys[:, g, 0:NT], in0=p2[:, 0:NT], scalar=pp[:, g, 0:1],
                                                            in1=sg[:, g, 0:NT], op0=ALU.mult, op1=ALU.mult),
                      reads=[p2, pp, sg], writes=[ys])
            branch_merge(0)

            cut(4, l == 0 and ti == 0)
            cut(14, l == 0 and ti == 4)
            silu_gates(1)
            wbg = wload(win_chunk(1))
            for c in range(4):
                p = ipf(wbg, c)
                kb.op(ACT, lambda h: h.copy(out=bgs[:, c, 0:NT], in_=p[:, 0:NT]), reads=[p], writes=[bgs])
            wcg = wload(win_chunk(2))
            for c in range(4):
                p = ipf(wcg, c)
                kb.op(ACT, lambda h: h.copy(out=cgs[:, c, 0:NT], in_=p[:, 0:NT]), reads=[p], writes=[cgs])
            whb = wload(win_chunk(3))
            for c in range(4):
                p = ipf(whb, c)
                seq, L = fill_seq(tA, c, 2, halo_p, stfm_p,
                                  lambda o: kb.op(DVE, lambda h: h.tensor_tensor(out=o, in0=v3(p[:, 0:NT], nbv), in1=v3(cgs[:, c, 0:NT], nbv), op=ALU.mult),
                                                  reads=[p, cgs], writes=[tA]))
                if want_state:
                    if samp:
                        kb.op(DVE, lambda h: h.tensor_copy(out=stnew[:, 1, c, 0:64].rearrange("p (b t) -> p b t", t=4), in_=seq[:, :, 2:6]),
                              reads=[tA], writes=[stnew])
                    else:
                        kb.op(DVE, lambda h: h.tensor_copy(out=stnew[:, 1, c, 0:2], in_=seq[:, 0, NT:NT + 2]), reads=[tA], writes=[stnew])
                acc = v3(tB[:, 0:NT], nbv)
                kb.op(DVE, lambda h: h.tensor_scalar(out=acc, in0=seq[:, :, 0:L], scalar1=pp[:, c, 1:2], scalar2=None, op0=ALU.mult),
                      reads=[tA, pp], writes=[tB])
                for j in (1, 2):
                    kb.op(DVE, lambda h: h.scalar_tensor_tensor(out=acc, in0=seq[:, :, j:j + L], scalar=pp[:, c, 1 + j:2 + j],
                                                                in1=acc, op0=ALU.mult, op1=ALU.add),
                          reads=[tA, tB, pp], writes=[tB])
                kb.op(DVE, lambda h: h.tensor_tensor(out=tC[:, 0:NT], in0=tB[:, 0:NT], in1=bgs[:, c, 0:NT], op=ALU.mult),
                      reads=[tB, bgs], writes=[tC])
                kb.op(DVE, lambda h: h.tensor_tensor(out=ys[:, c, 0:NT], in0=tC[:, 0:NT], in1=sg[:, c, 0:NT], op=ALU.mult),
                      reads=[tC, sg], writes=[ys])
            branch_merge(1)

            cut(5, l == 0 and ti == 0)
            cut(15, l == 0 and ti == 4)
            silu_gates(2)
            wval = wload(win_chunk(4))
            for c in range(4):
                p = ipf(wval, c)
                kb.op(ACT, lambda h: h.copy(out=cgs[:, c, 0:NT], in_=p[:, 0:NT]), reads=[p], writes=[cgs])
            wgl = wload(win_chunk(5))
            for c in range(4):
                p = ipf(wgl, c)
                kb.op(ACT, lambda h: h.activation(out=tC[:, 0:NT], in_=p[:, 0:NT], func=AF.Sigmoid), reads=[p], writes=[tC])
                seq, L = fill_seq(tA, c, 30, halo_g, stfm_g,
                                  lambda o: kb.op(DVE, lambda h: h.tensor_tensor(out=o, in0=v3(cgs[:, c, 0:NT], nbv), in1=v3(tC[:, 0:NT], nbv), op=ALU.mult),
                                                  reads=[cgs, tC], writes=[tA]))
                if want_state:
                    if samp:
                        kb.op(DVE, lambda h: h.tensor_copy(out=stnew[:, 2, c, 0:64].rearrange("p (b t) -> p b t", t=4), in_=seq[:, :, 30:34]),
                              reads=[tA], writes=[stnew])
                    else:
                        kb.op(DVE, lambda h: h.tensor_copy(out=stnew[:, 2, c, 0:30], in_=seq[:, 0, NT:NT + 30]), reads=[tA], writes=[stnew])
                acc = v3(cc[:, c, 0:NT], nbv)
                kb.op(DVE, lambda h: h.tensor_scalar(out=acc, in0=seq[:, :, 0:L], scalar1=pp[:, c, 4:5], scalar2=pp[:, c, 35:36],
                                                     op0=ALU.mult, op1=ALU.add), reads=[tA, pp], writes=[cc])
                for j in range(1, 31):
                    kb.op(DVE, lambda h: h.scalar_tensor_tensor(out=acc, in0=seq[:, :, j:j + L], scalar=pp[:, c, 4 + j:5 + j],
                                                                in1=acc, op0=ALU.mult, op1=ALU.add),
                          reads=[tA, cc, pp], writes=[cc])
            for c in range(4):
                kb.op(ACT, lambda h: h.copy(out=bgs[:, c, 0:NT], in_=cc[:, c, 0:NT]), reads=[cc], writes=[bgs])
            ps1 = next_pg()
            mm_group(ps1[:, 0:NT], [(onesb[:, :], bgs[:, c, 0:NT]) for c in range(4)], [onesb, bgs], ps1)
            kb.op(ACT, lambda h: h.activation(out=tD[:, 0:NT], in_=ps1[:, 0:NT], func=AF.Copy, scale=1.0 / BW), reads=[ps1], writes=[tD])
            for c in range(4):
                kb.op(DVE, lambda h: h.tensor_tensor(out=cc[:, c, 0:NT], in0=cc[:, c, 0:NT], in1=tD[:, 0:NT], op=ALU.subtract),
                      reads=[cc, tD], writes=[cc])
                kb.op(ACT, lambda h: h.activation(out=bgs[:, c, 0:NT], in_=cc[:, c, 0:NT], func=AF.Square), reads=[cc], writes=[bgs])
            ps2 = next_pg()
            mm_group(ps2[:, 0:NT], [(onesb[:, :], bgs[:, c, 0:NT]) for c in range(4)], [onesb, bgs], ps2)
            kb.op(ACT, lambda h: h.activation(out=tD[:, 0:NT], in_=ps2[:, 0:NT], func=AF.Sqrt, bias=epsb[:, 0:1], scale=1.0 / BW),
                  reads=[ps2, epsb], writes=[tD])
            kb.op(DVE, lambda h: h.reciprocal(out=tD[:, 0:NT], in_=tD[:, 0:NT]), reads=[tD], writes=[tD])
            for c in range(4):
                kb.op(DVE, lambda h: h.tensor_tensor(out=tC[:, 0:NT], in0=cc[:, c, 0:NT], in1=tD[:, 0:NT], op=ALU.mult),
                      reads=[cc, tD], writes=[tC])
                kb.op(ACT, lambda h: h.activation(out=tE[:, 0:NT], in_=tC[:, 0:NT], func=AF.Silu, bias=pp[:, c, 37:38], scale=pp[:, c, 36:37]),
                      reads=[tC, pp], writes=[tE])
                kb.op(DVE, lambda h: h.tensor_tensor(out=ys[:, c, 0:NT], in0=tE[:, 0:NT], in1=sg[:, c, 0:NT], op=ALU.mult),
                      reads=[tE, sg], writes=[ys])
            branch_merge(2)

            if want_state:
                for si, (npre, dp, ds_) in enumerate(((15, pool_p, pool_s), (2, sconv_p, sconv_s), (30, cconv_p, cconv_s))):
                    if samp:
                        keep = min(4, npre)
                        dsts = [(b * 4 + (4 - keep), keep, ds_[l, b, npre - keep:npre, :]) for b in range(NSB)]
                        emit_rows(lambda c: stnew[:, si, c, 0:64], 64, dsts)
                    else:
                        emit_rows(lambda c: stnew[:, si, c, 0:npre], npre, [(0, npre, dp[l])])

            cut(6, l == 0 and ti == 0)
            cut(16, l == 0 and ti == 4)
            silu_gates(3)
            in_attn[0] = True
            nlam = lamt[:, 2:3]
            O1, O2, R1, R2 = pacc
            if samp:
                kb.op(DVE, lambda h: h.memset(Qall[:], 0.0), writes=[Qall])
                for jm in range(2):
                    pr = slice(64 * jm, 64 * jm + 64)
                    kb.op(DVE, lambda h: h.tensor_copy(out=Qall[pr, :, :, jm, :],
                                                       in_=qT[pr, :, 0:64].rearrange("p h (b t) -> p h b t", t=4)),
                          reads=[qT], writes=[Qall])
                for b in range(NSB):
                    S_ps = next_psc()
                    for qtr in range(4):
                        hb = (b * 4 + qtr) % 2
                        kb.dma(POOL, [(lambda h, pgi=pgi: h.indirect_dma_start(
                            out=kpg[hb][:, pgi, :], out_offset=None, in_=ck[l],
                            in_offset=bass.IndirectOffsetOnAxis(ap=idx[:, b * 16 + qtr * 4 + pgi:b * 16 + qtr * 4 + pgi + 1], axis=0),
                            bounds_check=breg, oob_is_err=False)) for pgi in range(4)],
                            reads=[idx], writes=[kpg[hb]])
                        kb.dma(POOL, [(lambda h, pgi=pgi: h.indirect_dma_start(
                            out=vpg[hb][:, pgi, :], out_offset=None, in_=cv[l],
                            in_offset=bass.IndirectOffsetOnAxis(ap=idx[:, b * 16 + qtr * 4 + pgi:b * 16 + qtr * 4 + pgi + 1], axis=0),
                            bounds_check=breg, oob_is_err=False)) for pgi in range(4)],
                            reads=[idx], writes=[vpg[hb]])
                        for pgi in range(4):
                            if pgi % 2 == 0:
                                ptp = next_pg()
                                ptpb = ptp[:].bitcast(BF16)
                            for hh in range(4):
                                col = (pgi % 2) * 512 + hh * 128
                                transpose_to(ptpb[:, col:col + 128], kpg[hb][:, pgi, hh * 128:(hh + 1) * 128], identb[:, :],
                                             [kpg[hb], identb], ptp, inc=(pgi % 2 == 1 and hh == 3))
                            if pgi % 2 == 1:
                                kb.op(ACT, lambda h: h.copy(out=kTs[:, pgi - 1:pgi + 1, :, :],
                                                            in_=ptpb[:, 0:1024].rearrange("p (a h k) -> p a h k", a=2, h=4)),
                                      reads=[ptp], writes=[kTs])
                        for pgi in range(4):
                            page = qtr * 4 + pgi
                            for hh in range(4):
                                last = (pgi == 3 and hh == 3)
                                kb.op(PE, lambda h: h.matmul(S_ps[:, page * 32 + hh * 8:page * 32 + hh * 8 + 8], kTs[:, pgi, hh, :],
                                                             Qall[:, hh, b, :, :].rearrange("p j t -> p (j t)"), start=True, stop=True),
                                      reads=[kTs, Qall], writes=[S_ps], inc=last)
                        kb.op(ACT, lambda h: h.activation(out=pTs[:, qtr * 4:qtr * 4 + 4, :],
                                                          in_=S_ps[:, qtr * 128:(qtr + 1) * 128].rearrange("p (a c) -> p a c", c=32),
                                                          func=AF.Exp, scale=0.125), reads=[S_ps], writes=[pTs])
                        for pgi in range(4):
                            page = qtr * 4 + pgi
                            for hh in range(4):
                                kb.op(PE, lambda h: h.matmul(O1[:, hh * 8:hh * 8 + 8], vpg[hb][:, pgi, hh * 128:(hh + 1) * 128],
                                                             pTs[:, page, hh * 8:hh * 8 + 8], start=(page == 0 and hh == 0), stop=False),
                                      reads=[vpg[hb], pTs], writes=[O1], inc=False)
                            kb.op(PE, lambda h: h.matmul(R1[:, 0:32], onesb[:, :], pTs[:, page, :], start=(page == 0), stop=False),
                                  reads=[onesb, pTs], writes=[R1], inc=True)
                    S2 = next_psc()
                    for hh in range(4):
                        kb.op(PE, lambda h: h.matmul(S2[0:64, hh * 8:hh * 8 + 8], kT[:, hh, SEQ:SEQ + 64], Qall[:, hh, b, :, :].rearrange("p j t -> p (j t)"),
                                                     start=True, stop=True), reads=[kT, Qall], writes=[S2], inc=(hh == 3))
                    kb.op(ACT, lambda h: h.activation(out=pTn[0:64, :], in_=S2[0:64, 0:32], func=AF.Exp, scale=0.125), reads=[S2], writes=[pTn])
                    kb.op(DVE, lambda h: h.tensor_tensor(out=pTn[0:64, :].rearrange("p (h c) -> p h c", h=4), in0=pTn[0:64, :].rearrange("p (h c) -> p h c", h=4),
                                                         in1=smask[:, b * 8:(b + 1) * 8].unsqueeze(1).to_broadcast([64, 4, 8]), op=ALU.mult),
                          reads=[pTn, smask], writes=[pTn])
                    for hh in range(4):
                        kb.op(PE, lambda h: h.matmul(O1[:, hh * 8:hh * 8 + 8], vtok[0:64, 16, hh * 128:(hh + 1) * 128],
                                                     pTn[0:64, hh * 8:hh * 8 + 8], start=False, stop=True),
                              reads=[vtok, pTn], writes=[O1], inc=False)
                    kb.op(PE, lambda h: h.matmul(R1[:, 0:32], onesb[0:64, :], pTn[0:64, :], start=False, stop=True),
                          reads=[onesb, pTn], writes=[R1], inc=True)
                    kb.op(DVE, lambda h: h.reciprocal(out=rinv[:, :], in_=R1[:, 0:32]), reads=[R1], writes=[rinv])
                    kb.op(DVE, lambda h: h.tensor_tensor(out=rinv[:, :], in0=O1[:, 0:32], in1=rinv[:, :], op=ALU.mult), reads=[O1, rinv], writes=[rinv])
                    r4 = rinv[:, :].rearrange("p (h j t) -> p h j t", h=4, j=2)
                    kb.op(DVE, lambda h: h.scalar_tensor_tensor(out=oS[:, :, b * 4:(b + 1) * 4], in0=r4[:, :, 1, :], scalar=nlam, in1=r4[:, :, 0, :],
                                                                op0=ALU.mult, op1=ALU.add), reads=[rinv, lamt], writes=[oS])
            for hh in range(4):
                if not samp:
                    nkb = 4 * ti + 4
                    for kbi in range(nkb):
                        jd = kbi - 4 * ti
                        q0 = max(0, jd) * 128
                        for mp in range(2):
                            s_ps = next_psc()
                            pr = slice(64 * mp, 64 * mp + 64)
                            mm_group(s_ps[:, q0:NT], [(kT[pr, hh, kbi * 128:(kbi + 1) * 128], qT[pr, hh, q0:NT])], [kT, qT], s_ps)
                            pt_ = tb1 if mp == 0 else tb2
                            kb.op(ACT, lambda h: h.activation(out=pt_[:, q0:NT], in_=s_ps[:, q0:NT], func=AF.Exp, scale=0.125),
                                  reads=[s_ps], writes=[pt_])
                            if jd >= 0:
                                kb.op(DVE, lambda h: h.tensor_tensor(out=pt_[:, q0:q0 + 128], in0=pt_[:, q0:q0 + 128], in1=trib[:, :], op=ALU.mult),
                                      reads=[pt_, trib], writes=[pt_])
                            Oa, Ra = (O1, R1) if mp == 0 else (O2, R2)
                            kb.op(PE, lambda h: h.matmul(Oa[:, q0:NT], vtok[:, kbi, hh * 128:(hh + 1) * 128], pt_[:, q0:NT],
                                                         start=(kbi == 0), stop=(kbi == nkb - 1)), reads=[vtok, pt_], writes=[Oa], inc=False)
                            kb.op(PE, lambda h: h.matmul(Ra[:, q0:NT], onesb[:, :], pt_[:, q0:NT],
                                                         start=(kbi == 0), stop=(kbi == nkb - 1)), reads=[onesb, pt_], writes=[Ra], inc=True)
                    kb.op(DVE, lambda h: h.reciprocal(out=tC[:, 0:NT], in_=R1[:, 0:NT]), reads=[R1], writes=[tC])
                    kb.op(DVE, lambda h: h.reciprocal(out=tD[:, 0:NT], in_=R2[:, 0:NT]), reads=[R2], writes=[tD])
                    kb.op(DVE, lambda h: h.tensor_tensor(out=tC[:, 0:NT], in0=O1[:, 0:NT], in1=tC[:, 0:NT], op=ALU.mult), reads=[O1, tC], writes=[tC])
                    kb.op(DVE, lambda h: h.tensor_tensor(out=tD[:, 0:NT], in0=O2[:, 0:NT], in1=tD[:, 0:NT], op=ALU.mult), reads=[O2, tD], writes=[tD])
                    kb.op(DVE, lambda h: h.scalar_tensor_tensor(out=tE[:, 0:NT], in0=tD[:, 0:NT], scalar=nlam, in1=tC[:, 0:NT],
                                                                op0=ALU.mult, op1=ALU.add), reads=[tD, tC, lamt], writes=[tE])
                else:
                    kb.op(DVE, lambda h: h.tensor_copy(out=tE[:, 0:NT], in_=oS[:, hh, :]), reads=[oS], writes=[tE])
                kb.op(ACT, lambda h: h.activation(out=tb1[:, 0:NT], in_=tE[:, 0:NT], func=AF.Square), reads=[tE], writes=[tb1])
                pss = next_pg()
                mm_group(pss[:, 0:NT], [(onesb[:, :], tb1[:, 0:NT])], [onesb, tb1], pss)
                kb.op(ACT, lambda h: h.activation(out=tC[:, 0:NT], in_=pss[:, 0:NT], func=AF.Sqrt, bias=epsb[:, 0:1], scale=1.0 / 128),
                      reads=[pss, epsb], writes=[tC])
                kb.op(DVE, lambda h: h.reciprocal(out=tC[:, 0:NT], in_=tC[:, 0:NT]), reads=[tC], writes=[tC])
                kb.op(DVE, lambda h: h.scalar_tensor_tensor(out=tD[:, 0:NT], in0=tE[:, 0:NT], scalar=gsub[:, 0:1], in1=tC[:, 0:NT],
                                                            op0=ALU.mult, op1=ALU.mult), reads=[tE, gsub, tC], writes=[tD])
                kb.op(DVE, lambda h: h.scalar_tensor_tensor(out=ys[:, hh, 0:NT], in0=tD[:, 0:NT], scalar=1.0 - lam_init, in1=sg[:, hh, 0:NT],
                                                            op0=ALU.mult, op1=ALU.mult), reads=[tD, sg], writes=[ys])
            in_attn[0] = False
            branch_merge(3)

            cut(7, l == 0 and ti == 0)
            cut(17, l == 0 and ti == 4)
            for c in range(8):
                kb.op(ACT, lambda h: h.copy(out=hT[:, c, 0:NT], in_=merged[:, c, 0:NT]), reads=[merged], writes=[hT])
            wo = [wload(w_o[l][:, hf * 512:(hf + 1) * 512].rearrange("(kc p) c -> p kc c", p=128)) for hf in range(2)]
            for j in range(nblk):
                for hf in range(2):
                    p = next_pg()
                    mm_group(p[0:rows, :], [(hT[:, kc, j * 128:j * 128 + rows], wo[hf][:, kc, :]) for kc in range(8)], [hT, wo[hf]], p)
                    kb.op(ACT, lambda h: h.copy(out=tok1[0:rows, hf * 512:(hf + 1) * 512], in_=p[0:rows, :]), reads=[p], writes=[tok1])
                kb.op(DVE, lambda h: h.memset(small[:, 4:5], 0.0), writes=[small])
                kb.op(ACT, lambda h: h.activation(out=tok2[0:rows, :], in_=tok1[0:rows, :], func=AF.Square, accum_out=small[0:rows, 4:5]),
                      reads=[tok1, small], writes=[tok2, small])
                kb.op(ACT, lambda h: h.activation(out=small[0:rows, 5:6], in_=small[0:rows, 4:5], func=AF.Sqrt, bias=epsb[0:rows, 0:1], scale=1.0 / D),
                      reads=[small, epsb], writes=[small])
                kb.op(DVE, lambda h: h.reciprocal(out=small[0:rows, 6:7], in_=small[0:rows, 5:6]), reads=[small], writes=[small])
                kb.op(DVE, lambda h: h.scalar_tensor_tensor(out=tok1[0:rows, :], in0=tok1[0:rows, :], scalar=small[0:rows, 6:7], in1=gpost_b[0:rows, :],
                                                            op0=ALU.mult, op1=ALU.mult), reads=[tok1, small, gpost_b], writes=[tok1])
                gt = gate_s if samp else gate_p
                kb.op(DVE, lambda h: h.tensor_tensor(out=tok1[0:rows, :], in0=tok1[0:rows, :], in1=gt[0:rows, :], op=ALU.mult),
                      reads=[tok1, gt], writes=[tok1])
                kb.op(DVE, lambda h: h.tensor_tensor(out=xt[0:rows, j, :], in0=xt[0:rows, j, :], in1=tok1[0:rows, :], op=ALU.add),
                      reads=[xt, tok1], writes=[xt])
            cut(8, l == 0 and ti == 0)
            cut(9, l == 0 and ti == 3)
            cut(10, l == 0 and ti == 4)
            if l == DEPTH - 1:
                if samp:
                    kb.dma(SP, [lambda h: h.dma_start(out=y_s, in_=xt[0:64, 0, :])], reads=[xt], store=True)
                else:
                    kb.dma(SP, [lambda h: h.dma_start(out=y_p[ti * 512:(ti + 1) * 512, :].rearrange("(j p) d -> p j d", p=128), in_=xt[:])],
                           reads=[xt], store=True)
            else:
                if samp:
                    kb.dma(SP, [lambda h: h.dma_start(out=x1[SEQ:SEQ + 64, :], in_=xt[0:64, 0, :])], reads=[xt], writes=[x1buf], sem_of=xt)
                else:
                    kb.dma(SP, [lambda h: h.dma_start(out=x1[ti * 512:(ti + 1) * 512, :].rearrange("(j p) d -> p j d", p=128), in_=xt[:])],
                           reads=[xt], writes=[x1buf], sem_of=xt)
    return


def _consts():
    ident = np.eye(128, dtype=np.float32)
    tri = (np.arange(128)[:, None] <= np.arange(128)[None, :]).astype(np.float32)
    half = 32
    inv = np.power(np.float32(10000.0), -np.arange(half, dtype=np.float32) / np.float32(half)).astype(np.float32)
    cs = np.zeros((2, 17, 128, 32), np.float32)
    pos = np.arange(SEQ, dtype=np.float32)
    ang = (pos[:, None] * inv[None, :]).astype(np.float32)
    cs[0, :16] = np.cos(ang).reshape(16, 128, 32)
    cs[1, :16] = np.sin(ang).reshape(16, 128, 32)
    pos_s = (SEQ + (np.arange(64) % 4)).astype(np.float32)
    ang_s = (pos_s[:, None] * inv[None, :]).astype(np.float32)
    cs[0, 16, :64] = np.cos(ang_s)
    cs[1, 16, :64] = np.sin(ang_s)
    pinv = np.zeros((4, 16), np.float32)
    for g, w in enumerate(WINS):
        pinv[g] = 1.0 / np.minimum(w, np.arange(16) + 1)
    key = np.arange(64)
    col = np.arange(128)
    cb, ct = col // 8, col % 4
    smask = ((key[:, None] // 4 == cb[None, :]) & (key[:, None] % 4 <= ct[None, :])).astype(np.float32)
    return ident, tri, cs, pinv, smask


_CACHE = {}


def kernel(**inputs):
    f = lambda a: np.ascontiguousarray(np.asarray(a))
    cache_k = f(inputs["cache_k"]); cache_v = f(inputs["cache_v"])
    n_pool = cache_k.shape[1]
    if n_pool not in _CACHE:
        _CACHE[n_pool] = build(n_pool)
    nc, _es = _CACHE[n_pool]
    ident, tri, cs, pinv, smask = _consts()
    ck = cache_k.reshape(DEPTH, n_pool * 128, BW)
    cv = cache_v.reshape(DEPTH, n_pool * 128, BW)
    ppack = np.concatenate([f(inputs["pool_scale"])[:, None, :], f(inputs["w_sconv"]), f(inputs["w_cconv"]),
                            f(inputs["b_cconv"])[:, None, :], f(inputs["g_cnorm"])[:, None, :],
                            f(inputs["b_cnorm"])[:, None, :]], axis=1).astype(np.float32)
    shared = {
        "ck0": ck[0], "ck1": ck[1], "cv0": cv[0], "cv1": cv[1], "w_ada": f(inputs["w_ada"]), "b_ada": f(inputs["b_ada"]),
        "g_pre": f(inputs["g_pre"]), "g_post": f(inputs["g_post"]), "w_in": f(inputs["w_in"]),
        "w_pool": f(inputs["w_pool"]), "ppack": np.ascontiguousarray(ppack),
        "lam_qk": f(inputs["lambda_qk"]).reshape(DEPTH, 256), "g_subln": f(inputs["g_subln"]),
        "w_branch": f(inputs["w_branch"]), "w_o": f(inputs["w_o"]),
        "c_ident": ident, "c_tri": tri, "c_cs": cs, "c_pinv": pinv, "c_smask": smask,
    }
    x_prompt = f(inputs["x_prompt"]); x_sample = f(inputs["x_sample"])
    c_prompt = f(inputs["c_prompt"]); c_sample = f(inputs["c_sample"])
    page_table = f(inputs["page_table"]).astype(np.int32)
    st_pool = f(inputs["state_pool"]); st_sconv = f(inputs["state_sconv"]); st_cconv = f(inputs["state_cconv"])
    in_maps = []
    for c in range(8):
        sb = slice(NSB * c, NSB * (c + 1))
        m = dict(shared)
        m["xp"] = x_prompt[c]
        m["xs"] = np.ascontiguousarray(x_sample[sb].reshape(NS, D))
        m["ptab"] = np.ascontiguousarray(page_table[sb])
        m["st_pool"] = np.ascontiguousarray(st_pool[:, sb]); m["st_sconv"] = np.ascontiguousarray(st_sconv[:, sb])
        m["st_cconv"] = np.ascontiguousarray(st_cconv[:, sb])
        m["c_all"] = np.ascontiguousarray(np.concatenate([c_prompt[c:c + 1], c_sample[sb]], axis=0))
        in_maps.append(m)
    res = run_bass_kernel_spmd(nc, in_maps, core_ids=list(range(8))).results
    g = lambda name: [np.asarray(r[name]) for r in res]
    y_prompt = np.stack(g("y_p"), 0)
    y_sample = np.concatenate(g("y_s"), 0).reshape(8 * NSB, 4, D)
    k_prompt = np.stack(g("k_p"), 1).reshape(DEPTH, 8, SEQ, 4, 2, 64)
    v_prompt = np.stack(g("v_p"), 1).reshape(DEPTH, 8, SEQ, 4, 128)
    pool_prompt = np.stack(g("pool_p"), 1)
    sconv_prompt = np.stack(g("sconv_p"), 1)
    cconv_prompt = np.stack(g("cconv_p"), 1)
    k_sample = np.concatenate([a.reshape(DEPTH, NSB, 4, 4, 2, 64) for a in g("k_s")], 1)
    v_sample = np.concatenate([a.reshape(DEPTH, NSB, 4, 4, 128) for a in g("v_s")], 1)
    pool_sample = np.concatenate(g("pool_s"), 1)
    sconv_sample = np.concatenate(g("sconv_s"), 1)
    cconv_sample = np.concatenate(g("cconv_s"), 1)
    outs = (y_prompt, y_sample, k_prompt, v_prompt, pool_prompt, sconv_prompt, cconv_prompt,
            k_sample, v_sample, pool_sample, sconv_sample, cconv_sample)
    return tuple(np.ascontiguousarray(o, dtype=np.float32) for o in outs)
```
